# Optimizing a Trainium2 kernel written in Bass

```python
import math
import jax, jax.numpy as jnp
from jax import lax
import numpy as np

D_MODEL = 2048
BATCH = 4
SEQ = 2048
DEPTH = 2
DEC_BATCH = 128
DEC_SEQ = 1
PAST_LEN = 16384
PAGE_SIZE = 128

N_EVEN = (DEPTH + 1) // 2
N_ODD = DEPTH // 2
D_A = D_MODEL // 2
D_B = D_MODEL // 2
K_A = 31
K_B = 3
D_C = D_MODEL
H_C = 8
DH_C = D_C // H_C
CHUNK = 128
D_FF = 11 * D_MODEL // 4
D_AB_IN = 2 * D_A + 3 * D_B
EPS = 1e-6

kernel_name = "macaron_conv_gmlp_hybrid_step"


def rmsnorm(x, g):
    xf = x.astype(jnp.float32)
    y = xf * lax.rsqrt(jnp.mean(xf * xf, axis=-1, keepdims=True) + EPS)
    return (y * g.astype(jnp.float32)).astype(x.dtype)


def layernorm(x, g, b):
    xf = x.astype(jnp.float32)
    mu = jnp.mean(xf, axis=-1, keepdims=True)
    xc = xf - mu
    y = xc * lax.rsqrt(jnp.mean(xc * xc, axis=-1, keepdims=True) + EPS)
    return (y * g.astype(jnp.float32) + b.astype(jnp.float32)).astype(x.dtype)


def swiglu(x, w1, w3, w2):
    return (jax.nn.silu(x @ w1) * (x @ w3)) @ w2


def causal_depthwise(rows, prefix, w):
    full = jnp.concatenate([prefix.astype(rows.dtype), rows], axis=1)
    out = lax.conv_general_dilated(
        full, w[:, None, :].astype(rows.dtype), window_strides=(1,), padding='VALID',
        dimension_numbers=('NWC', 'WIO', 'NWC'), feature_group_count=rows.shape[-1])
    return out, full[:, -(w.shape[0] - 1):]


def conv_mixers(h, st_a, st_b, w_in, a_conv_w, a_conv_b, a_ln_g, a_ln_b, b_conv_w, w_out):
    p = h @ w_in
    pa, ga, bg, cg, hb = jnp.split(
        p, [D_A, 2 * D_A, 2 * D_A + D_B, 2 * D_A + 2 * D_B], axis=-1)
    a = pa * jax.nn.sigmoid(ga)
    a_c, st_a_new = causal_depthwise(a, st_a, a_conv_w)
    a_out = jax.nn.silu(layernorm(a_c + a_conv_b, a_ln_g, a_ln_b))
    b_c, st_b_new = causal_depthwise(cg * hb, st_b, b_conv_w)
    b_out = bg * b_c
    y = jnp.concatenate([a_out, b_out], axis=-1) @ w_out
    return y, st_a_new, st_b_new


def chunk_gmlp(h, w_in, b_in, ln_g, ln_b, w_s, b_s, w_out, chunk_len):
    bsz, t, _ = h.shape
    z = jax.nn.gelu(h @ w_in + b_in, approximate=False)
    u, v = jnp.split(z, 2, axis=-1)
    v = layernorm(v, ln_g, ln_b)
    L = chunk_len
    n = t // L
    mask = jnp.tril(jnp.ones((L, L), dtype=bool))
    ws = jnp.where(mask[None], w_s[:, :L, :L], 0).astype(v.dtype)
    vc = v.reshape(bsz, n, L, H_C, DH_C)
    s = jnp.einsum('hts,bnshd->bnthd', ws, vc) + b_s[:, :L].T[None, None, :, :, None].astype(v.dtype)
    y = (u * s.reshape(bsz, t, D_C)) @ w_out
    return y, v


def trunk(x, conv_a_prev, conv_b_prev, chunk_len, norm_g, ffn_w1, ffn_w3, ffn_w2,
          ab_w_in, a_conv_w, a_conv_b, a_ln_g, a_ln_b, b_conv_w, ab_w_out,
          c_w_in, c_b_in, c_ln_g, c_ln_b, c_w_s, c_b_s, c_w_out, final_g):
    new_a, new_b, new_v = [], [], []
    for i in range(DEPTH):
        x = x + 0.5 * swiglu(rmsnorm(x, norm_g[i, 0]), ffn_w1[i, 0], ffn_w3[i, 0], ffn_w2[i, 0])
        hn = rmsnorm(x, norm_g[i, 1])
        if i % 2 == 0:
            j = i // 2
            m, sa, sb = conv_mixers(hn, conv_a_prev[j], conv_b_prev[j], ab_w_in[j], a_conv_w[j],
                                    a_conv_b[j], a_ln_g[j], a_ln_b[j], b_conv_w[j], ab_w_out[j])
            new_a.append(sa)
            new_b.append(sb)
        else:
            j = i // 2
            m, v = chunk_gmlp(hn, c_w_in[j], c_b_in[j], c_ln_g[j], c_ln_b[j], c_w_s[j],
                              c_b_s[j], c_w_out[j], chunk_len)
            new_v.append(v)
        x = x + m
        x = x + 0.5 * swiglu(rmsnorm(x, norm_g[i, 2]), ffn_w1[i, 1], ffn_w3[i, 1], ffn_w2[i, 1])
    return rmsnorm(x, final_g), jnp.stack(new_a), jnp.stack(new_b), jnp.stack(new_v)


def setup_inputs(seed: int = 0) -> dict:
    key = jax.random.key(seed)
    ks = jax.random.split(key, 24)
    f32 = jnp.float32
    nrm = lambda k, shape, s: jax.random.normal(k, shape, f32) * s
    return {
        "x_prompt": nrm(ks[0], (BATCH, SEQ, D_MODEL), 1.0),
        "x_sample": nrm(ks[1], (DEC_BATCH, DEC_SEQ, D_MODEL), 1.0),
        "state_conv_a": nrm(ks[2], (N_EVEN, DEC_BATCH, K_A - 1, D_A), 0.5),
        "state_conv_b": nrm(ks[3], (N_EVEN, DEC_BATCH, K_B - 1, D_B), 0.5),
        "norm_g": 1.0 + nrm(ks[4], (DEPTH, 3, D_MODEL), 0.01),
        "ffn_w1": nrm(ks[5], (DEPTH, 2, D_MODEL, D_FF), D_MODEL ** -0.5),
        "ffn_w3": nrm(ks[6], (DEPTH, 2, D_MODEL, D_FF), D_MODEL ** -0.5),
        "ffn_w2": nrm(ks[7], (DEPTH, 2, D_FF, D_MODEL), D_FF ** -0.5),
        "ab_w_in": nrm(ks[8], (N_EVEN, D_MODEL, D_AB_IN), D_MODEL ** -0.5),
        "a_conv_w": nrm(ks[9], (N_EVEN, K_A, D_A), K_A ** -0.5),
        "a_conv_b": nrm(ks[10], (N_EVEN, D_A), 0.01),
        "a_ln_g": 1.0 + nrm(ks[11], (N_EVEN, D_A), 0.01),
        "a_ln_b": nrm(ks[12], (N_EVEN, D_A), 0.01),
        "b_conv_w": nrm(ks[13], (N_EVEN, K_B, D_B), K_B ** -0.5),
        "ab_w_out": nrm(ks[14], (N_EVEN, D_A + D_B, D_MODEL), (D_A + D_B) ** -0.5),
        "c_w_in": nrm(ks[15], (N_ODD, D_MODEL, 2 * D_C), D_MODEL ** -0.5),
        "c_b_in": nrm(ks[16], (N_ODD, 2 * D_C), 0.01),
        "c_ln_g": 1.0 + nrm(ks[17], (N_ODD, D_C), 0.01),
        "c_ln_b": nrm(ks[18], (N_ODD, D_C), 0.01),
        "c_w_s": nrm(ks[19], (N_ODD, H_C, CHUNK, CHUNK), CHUNK ** -0.5),
        "c_b_s": 1.0 + nrm(ks[20], (N_ODD, H_C, CHUNK), 0.01),
        "c_w_out": nrm(ks[21], (N_ODD, D_C, D_MODEL), D_C ** -0.5),
        "final_g": 1.0 + nrm(ks[22], (D_MODEL,), 0.01),
    }


def reference(x_prompt, x_sample, state_conv_a, state_conv_b, norm_g, ffn_w1, ffn_w3, ffn_w2,
              ab_w_in, a_conv_w, a_conv_b, a_ln_g, a_ln_b, b_conv_w, ab_w_out,
              c_w_in, c_b_in, c_ln_g, c_ln_b, c_w_s, c_b_s, c_w_out, final_g):
    bsz = x_prompt.shape[0]
    zero_a = jnp.zeros((N_EVEN, bsz, K_A - 1, D_A), x_prompt.dtype)
    zero_b = jnp.zeros((N_EVEN, bsz, K_B - 1, D_B), x_prompt.dtype)
    y_prompt, new_conv_a_prompt, new_conv_b_prompt, _ = trunk(
        x_prompt, zero_a, zero_b, CHUNK, norm_g, ffn_w1, ffn_w3, ffn_w2,
        ab_w_in, a_conv_w, a_conv_b, a_ln_g, a_ln_b, b_conv_w, ab_w_out,
        c_w_in, c_b_in, c_ln_g, c_ln_b, c_w_s, c_b_s, c_w_out, final_g)
    y_sample, new_conv_a_sample, new_conv_b_sample, new_chunk_v_sample = trunk(
        x_sample, state_conv_a, state_conv_b, x_sample.shape[1], norm_g, ffn_w1, ffn_w3, ffn_w2,
        ab_w_in, a_conv_w, a_conv_b, a_ln_g, a_ln_b, b_conv_w, ab_w_out,
        c_w_in, c_b_in, c_ln_g, c_ln_b, c_w_s, c_b_s, c_w_out, final_g)
    return (y_prompt, y_sample, new_conv_a_prompt, new_conv_b_prompt,
            new_conv_a_sample, new_conv_b_sample, new_chunk_v_sample)
```

```python
from contextlib import ExitStack
import numpy as np
import concourse.bass as bass
import concourse.mybir as mybir
from concourse.bass_utils import run_bass_kernel_spmd

F32 = mybir.dt.float32
F32R = mybir.dt.float32r
BF16 = mybir.dt.bfloat16
ALU = mybir.AluOpType
AF = mybir.ActivationFunctionType

ENGS = ("pe", "act", "dve", "pool", "sp")
NCORES = 8
D = 2048
DFF = 5632
W = 1072
HALO = 32
NPR = 1024
NSM = 16
TOUT = NPR + NSM
EPS = 1e-6
DEBUG_STOP = None


def CALL(name, *a, **kw):
    return (name, a, kw)


class Op:
    __slots__ = ("eng", "fn", "deps", "needed", "sig", "dma", "slot", "val")


class Sched:
    def __init__(self):
        self.ops = {e: [] for e in ENGS}
        self.lastw = {}
        self.readers = {}
        self.slot_cnt = {}
        self.last_dma = {}
        self.n = 0

    def add(self, eng, fn, reads=(), writes=(), slot=None, extra_deps=()):
        o = Op()
        o.eng = eng
        o.fn = fn
        o.needed = False
        o.sig = 0
        o.dma = slot is not None
        o.slot = slot
        o.val = 0
        if o.dma:
            c = self.slot_cnt.get(slot, 0) + 1
            self.slot_cnt[slot] = c
            o.val = 16 * c
            self.last_dma[slot] = o
        deps = {}
        lastw = self.lastw
        readers = self.readers
        for t in reads:
            w = lastw.get(t)
            if w is not None:
                deps[id(w)] = w
        for t in writes:
            w = lastw.get(t)
            if w is not None:
                deps[id(w)] = w
            r = readers.get(t)
            if r:
                for x in r.values():
                    deps[id(x)] = x
        for d in extra_deps:
            deps[id(d)] = d
        dl = []
        for d in deps.values():
            if d is o:
                continue
            if (not d.dma) and (not o.dma) and d.eng == "pe" and eng == "pe":
                continue
            d.needed = True
            dl.append(d)
        o.deps = dl
        rkey = ("q", self.n) if o.dma else eng
        for t in reads:
            r = readers.get(t)
            if r is None:
                readers[t] = {rkey: o}
            else:
                r[rkey] = o
        for t in writes:
            lastw[t] = o
            readers[t] = None
        self.ops[eng].append(o)
        self.n += 1
        return o

    def barrier(self, engs=("pe", "act", "dve"), dma_slots=()):
        last = {}
        for e in engs:
            last[e] = None
            for o in reversed(self.ops[e]):
                if o.fn is not None:
                    last[e] = o
                    break
        dd = [self.last_dma[k] for k in dma_slots if k in self.last_dma]
        for e in engs:
            ds = [last[x] for x in engs if x != e and last[x] is not None]
            self.add(e, None, extra_deps=ds + dd)

    def emit(self, nc, final_slots=()):
        for e in ENGS:
            k = 0
            for o in self.ops[e]:
                if o.needed and not o.dma:
                    k += 1
                    o.sig = k
        stats = {"waits": 0, "ins": 0}
        with ExitStack() as st:
            esem = {e: st.enter_context(nc.semaphore("s_" + e)) for e in ENGS}
            ssem = {}
            for i, k in enumerate(self.slot_cnt):
                ssem[k] = st.enter_context(nc.semaphore("d%d" % i))
            block = st.enter_context(nc.Block())

            def run(ename, eng):
                waited = {}
                for o in self.ops[ename]:
                    need = {}
                    for d in o.deps:
                        if d.dma:
                            key = ("d", d.slot)
                            val = d.val
                        else:
                            key = ("e", d.eng)
                            val = d.sig
                        if need.get(key, 0) < val:
                            need[key] = val
                    for key, val in need.items():
                        if waited.get(key, 0) >= val:
                            continue
                        waited[key] = val
                        sem = ssem[key[1]] if key[0] == "d" else esem[key[1]]
                        eng.wait_ge(sem, val)
                        stats["waits"] += 1
                    if o.fn is None:
                        continue
                    ins = getattr(eng, o.fn[0])(*o.fn[1], **o.fn[2])
                    stats["ins"] += 1
                    if o.dma:
                        ins.then_inc(ssem[o.slot], 16)
                    elif o.needed:
                        ins.then_inc(esem[ename], 1)
                if ename == "sp":
                    for k in final_slots:
                        eng.wait_ge(ssem[k], 16 * self.slot_cnt[k])

            @block.tensor
            def _(e):
                run("pe", e)

            @block.scalar
            def _(e):
                run("act", e)

            @block.vector
            def _(e):
                run("dve", e)

            @block.gpsimd
            def _(e):
                run("pool", e)

            @block.sync
            def _(e):
                run("sp", e)
        return stats


_GB = [0, 32, 252, 264, 268, 284, 348, 358, 380, 512, 520, 528, 536, 544, 568, 716, 726, 788, 800, 804, 808, 820, 1024, 1048, 1056, 1064, 1072]


def gran(a, b):
    return [i for i in range(len(_GB) - 1) if _GB[i] < b and _GB[i + 1] > a]


def T(name, k, a, b):
    return [(name, k, g) for g in gran(a, b)]


P_NG = 0
P_FG = 96
P_ACW = 112
P_ACB = 360
P_ALG = 368
P_ALB = 376
P_BCW = 384
P_CBI = 408
P_CLG = 440
P_CLB = 456
P_N = 472
B_W00 = 0
B_BS0 = 8
B_BS = 16
B_N = 16 + 1024


def build_program(stop=None):
    nc = bass.Bass("TRN2", target_bir_lowering=False)

    def din(name, shape):
        return nc.dram_tensor(name, shape, F32, kind="ExternalInput").ap()

    def dout(name, shape):
        return nc.dram_tensor(name, shape, F32, kind="ExternalOutput").ap()

    xT = din("xT", [D, W])
    sa = din("sa", [1024, 30, NSM])
    sb = din("sb", [1024, 2, NSM])
    w1 = din("w1", [4, D, DFF])
    w3 = din("w3", [4, D, DFF])
    w2 = din("w2", [4, DFF, D])
    abin = din("abin", [D, 5120])
    about = din("about", [D, D])
    cin = din("cin", [D, 4096])
    cout = din("cout", [D, D])
    prm = din("prm", [128, P_N])
    wst = din("wst", [128, 1024])
    brow = din("brow", [1, B_N])
    yT = dout("yT", [D, TOUT])
    caP = dout("caP", [1024, 30])
    cbP = dout("cbP", [1024, 2])
    caS = dout("caS", [1024, 30, NSM])
    cbS = dout("cbS", [1024, 2, NSM])
    vS = dout("vS", [D, NSM])

    def kview(ap2d):
        return ap2d.rearrange("(c p) n -> p c n", p=128)

    w1v = [kview(w1[f]) for f in range(4)]
    w3v = [kview(w3[f]) for f in range(4)]
    w2v = [kview(w2[f]) for f in range(4)]
    abinv = kview(abin)
    aboutv = kview(about)
    cinv = kview(cin)
    coutv = kview(cout)
    xTv = kview(xT)
    yTv = kview(yT)
    vSv = kview(vS)

    S = Sched()
    AW = 53000
    arena = nc.alloc_sbuf_tensor("arena", [128, AW], F32)
    ps = [nc.alloc_psum_tensor("ps%d" % i, [128, 512], F32) for i in range(8)]
    cur = [0]

    def alloc(nwords):
        o = cur[0]
        cur[0] += (nwords + 7) // 8 * 8
        assert cur[0] <= AW, ("arena overflow", cur[0])
        return o

    def vf32(off, n):
        return arena[:, off:off + n]

    def vbf(off, nwords):
        return arena[:, off:off + nwords].bitcast(BF16)

    o_X = alloc(16 * W)
    X = vf32(o_X, 16 * W).rearrange("p (c t) -> p c t", t=W)
    o_HN = alloc(8 * W)
    HN = vbf(o_HN, 8 * W).rearrange("p (c t) -> p c t", t=W)
    WS = []
    for i in range(4):
        o = alloc(2048)
        WS.append(vbf(o, 2048).rearrange("p (c n) -> p c n", n=256))
    PRM = vf32(alloc(P_N), P_N)
    o_BROW = cur[0]
    BROW = vf32(alloc(1072), 1072)[:, 0:B_N]
    ONES = vf32(alloc(128), 128)
    IDF = vf32(alloc(128), 128)
    IDB = vbf(alloc(64), 64)
    RSTD = vf32(alloc(W), W)
    o_SQ = alloc(1024)
    SQ = [vf32(o_SQ, 512), vf32(o_SQ + 512, 512)]
    WSTB = vbf(alloc(512), 512).rearrange("p (h t) -> p h t", t=128)
    WSSB = vbf(alloc(64), 64).rearrange("p (h t) -> p h t", t=16)
    BSBS = vf32(alloc(128), 128).rearrange("p (h t) -> p h t", t=16)
    VS = vf32(alloc(256), 256).rearrange("p (c b) -> p c b", b=16)
    o_FULLS = cur[0]
    FULLS = vf32(alloc(31 * 16), 31 * 16).rearrange("p (k b) -> p k b", b=16)
    FULLSB = vf32(alloc(48), 48).rearrange("p (k b) -> p k b", b=16)
    ACARRY = vf32(alloc(256), 256).rearrange("p (c t) -> p c t", t=32)
    CHCARRY = vf32(alloc(256), 256).rearrange("p (c t) -> p c t", t=32)
    EPSC = vf32(alloc(8), 8)
    prm_eps = EPSC[:, 0:1]
    o_R = cur[0]
    RW = AW - o_R
    LNB = {"mu": vf32(o_BROW, 1072)[:, 0:536], "rs": vf32(o_BROW, 1072)[:, 536:1072], "name": "lnb0"}
    LNB1 = {"mu": vf32(o_FULLS, 1056)[:, 0:528], "rs": vf32(o_FULLS, 1056)[:, 528:1056], "name": "lnb1"}
    LNC = [LNB]

    bank_i = [0]

    def nb():
        b = bank_i[0] % 6
        bank_i[0] += 1
        return b

    def prm_col(c):
        return PRM[:, c:c + 1]

    slot_i = [0]

    def load_w(src_ap, nk):
        sl = slot_i[0] % 4
        dst = WS[sl][:, 0:nk, :]
        xd = [S.last_dma[("x", 2)]] if (slot_i[0] == 2 and ("x", 2) in S.last_dma) else []
        slot_i[0] += 1
        S.add("pool", CALL("dma_start", out=dst, in_=src_ap), writes=[("ws", sl)], slot=("ws", sl), extra_deps=xd)
        return sl

    defer = [None, {}]

    def mm_group(bank, tn, sl, m, nk, rhs_of, rhs_toks, stat_tile=None):
        pst = ps[bank][:, 0:tn]
        for k in range(nk):
            lhsT = WS[sl][:, k, m * 128:(m + 1) * 128]
            rhs = rhs_of(k)
            S.add("pe", CALL("matmul", pst, lhsT=lhsT, rhs=rhs, start=(k == 0), stop=(k == nk - 1)),
                  reads=[("ws", sl)] + rhs_toks(k), writes=[("ps", bank)])
        if defer[0] is not None:
            t = defer[0]
            defer[0] = None
            rstd_begin(t)
        if stat_tile is not None and stat_tile[0] in defer[1]:
            del defer[1][stat_tile[0]]
            rstd_begin([stat_tile])

    def rstd_defer(tiles, per_tile=False):
        if per_tile:
            defer[1] = {t0: tn for (t0, tn) in tiles}
        else:
            defer[0] = tiles

    S.add("sp", CALL("dma_start", out=PRM, in_=prm[:, :]), writes=["prm"], slot="prm")
    S.add("sp", CALL("dma_start", out=BROW, in_=brow.partition_broadcast(128)), writes=["brow"], slot="brow")
    for ti, (t0, tn) in enumerate([(0, 358), (358, 358), (716, 356)]):
        wr = []
        for k in range(16):
            wr += T("x", k, t0, t0 + tn)
        S.add("sp", CALL("dma_start", out=X[:, :, t0:t0 + tn], in_=xTv[:, :, t0:t0 + tn]), writes=wr, slot=("x", ti))
    S.add("dve", CALL("memset", ONES, 1.0), writes=["ones"])
    S.add("dve", CALL("memset", IDF, 1.0), writes=["idf"])
    S.add("pool", CALL("affine_select", out=IDF, in_=IDF, pattern=[[1, 128]], compare_op=ALU.is_equal, fill=0.0,
                                            base=0, channel_multiplier=-1), reads=["idf"], writes=["idf"])
    S.add("dve", CALL("tensor_copy", out=IDB, in_=IDF), reads=["idf"], writes=["idb"])
    def colsum(bank, tn, nk, src_of, src_toks, square):
        for k in range(nk):
            if square:
                sq = SQ[k % 2]
                src = src_of(k)
                sqr = sq[:, 0:tn]
                S.add("act", CALL("activation", out=sqr, in_=src, func=AF.Square),
                      reads=src_toks(k), writes=[("sq", k % 2)])
                S.add("pe", CALL("matmul", ps[bank][:, 0:tn], lhsT=ONES, rhs=sqr, start=(k == 0), stop=(k == nk - 1)),
                      reads=["ones", ("sq", k % 2)], writes=[("ps", bank)])
            else:
                S.add("pe", CALL("matmul", ps[bank][:, 0:tn], lhsT=ONES, rhs=src_of(k), start=(k == 0), stop=(k == nk - 1)),
                      reads=["ones"] + src_toks(k), writes=[("ps", bank)])

    def rstd_from(src, src_toks, dst, dst_toks, inv_n):
        S.add("act", CALL("activation", out=dst, in_=src, func=AF.Sqrt, bias=prm_eps, scale=inv_n),
              reads=src_toks + ["eps"], writes=dst_toks)
        S.add("dve", CALL("reciprocal", out=dst, in_=dst), reads=dst_toks, writes=dst_toks)

    S.add("dve", CALL("memset", EPSC, EPS), writes=["eps"])

    sq_i = [0]
    pend = []

    def zero_rstd(a, b):
        S.add("dve", CALL("memset", RSTD[:, a:b], 0.0), writes=T("rstd", 0, a, b))

    def flush_one():
        q, g0, n = pend.pop(0)
        S.add("dve", CALL("tensor_tensor", out=RSTD[:, g0:g0 + n], in0=RSTD[:, g0:g0 + n], in1=SQ[q][:, 0:n], op=ALU.add),
              reads=[("sq", q)] + T("rstd", 0, g0, g0 + n), writes=T("rstd", 0, g0, g0 + n))

    def flush_all():
        while pend:
            flush_one()

    def accum_sq(src, src_toks, g0, n):
        q = sq_i[0] % 2
        sq_i[0] += 1
        S.add("act", CALL("activation", out=SQ[q][:, 0:n], in_=src, func=AF.Square), reads=src_toks, writes=[("sq", q)])
        pend.append((q, g0, n))
        if len(pend) > 1:
            flush_one()

    def rmsnorm(gcol, tiles, out_of, out_toks, fused):
        if not fused:
            zero_rstd(tiles[0][0], W)
            for (t0, tn) in tiles:
                for k in range(16):
                    accum_sq(X[:, k, t0:t0 + tn], T("x", k, t0, t0 + tn), t0, tn)
            flush_all()
        for (t0, tn) in tiles:
            bank = nb()
            S.add("pe", CALL("matmul", ps[bank][:, 0:tn], lhsT=ONES, rhs=RSTD[:, t0:t0 + tn], start=True, stop=True),
                  reads=["ones"] + T("rstd", 0, t0, t0 + tn), writes=[("ps", bank)])
            rstd_from(ps[bank][:, 0:tn], [("ps", bank)], RSTD[:, t0:t0 + tn], T("rstd", 0, t0, t0 + tn), 1.0 / D)
            for k in range(16):
                o = out_of(k, t0, tn)
                S.add("dve", CALL("scalar_tensor_tensor", out=o, in0=X[:, k, t0:t0 + tn], scalar=prm_col(gcol + k),
                                  in1=RSTD[:, t0:t0 + tn], op0=ALU.mult, op1=ALU.mult),
                      reads=T("x", k, t0, t0 + tn) + T("rstd", 0, t0, t0 + tn) + ["prm"], writes=out_toks(k, t0, tn))

    def norm_to_hn(gcol, tiles, fused):
        rmsnorm(gcol, tiles, lambda k, t0, tn: HN[:, k, t0:t0 + tn], lambda k, t0, tn: T("hn", k, t0, t0 + tn), fused)

    G = vbf(o_R, 8 * W).rearrange("p (c t) -> p c t", t=W)
    assert 8 * W <= RW

    def rstd_begin(tiles):
        for i, (t0, tn) in enumerate(tiles):
            bank = 6 + i % 2
            S.add("pe", CALL("matmul", ps[bank][:, 0:tn], lhsT=ONES, rhs=RSTD[:, t0:t0 + tn], start=True, stop=True),
                  reads=["ones"] + T("rstd", 0, t0, t0 + tn), writes=[("ps", bank)])
            rstd_from(ps[bank][:, 0:tn], [("ps", bank)], RSTD[:, t0:t0 + tn], T("rstd", 0, t0, t0 + tn), 1.0 / D)

    def hn_xg(oc, g0, n, gcol):
        S.add("act", CALL("mul", out=HN[:, oc, g0:g0 + n], in_=X[:, oc, g0:g0 + n], mul=prm_col(gcol + oc)),
              reads=T("x", oc, g0, g0 + n) + ["prm"], writes=T("hn", oc, g0, g0 + n))

    def post_mixer_epilogue(norm_idx, half, oc, gt, tn):
        hn_xg(oc, gt, tn, P_NG + norm_idx * 16)
        if half == 1:
            accum_sq(X[:, oc, gt:gt + tn], T("x", oc, gt, gt + tn), gt, tn)

    def setup_spatial():
        WSTF = vf32(o_FULLS, 1024)
        S.add("sp", CALL("dma_start", out=WSTF, in_=wst[:, :]), writes=["wstf"], slot="wstf")
        S.add("pool", CALL("affine_select", out=WSTF.rearrange("p (h t) -> p h t", t=128), in_=WSTF.rearrange("p (h t) -> p h t", t=128),
                                                pattern=[[0, 8], [1, 128]], compare_op=ALU.is_ge, fill=0.0, base=0,
                                                channel_multiplier=-1), reads=["wstf"], writes=["wstf"])
        S.add("dve", CALL("tensor_copy", out=WSTB.rearrange("p h t -> p (h t)"), in_=WSTF), reads=["wstf"], writes=["wstb"])
        for h in range(8):
            S.add("dve", CALL("tensor_scalar", out=WSSB[0:16, h, :], in0=IDF[0:16, 0:16], scalar1=BROW[0:16, B_W00 + h:B_W00 + h + 1],
                                                        scalar2=None, op0=ALU.mult), reads=["idf", "brow"], writes=[("wssb", h)])
            S.add("dve", CALL("tensor_scalar", out=BSBS[:, h, :], in0=ONES[:, 0:16], scalar1=BROW[:, B_BS0 + h:B_BS0 + h + 1],
                                                        scalar2=None, op0=ALU.mult), reads=["ones", "brow"], writes=[("bsbs", h)])

    def ffn(f, tiles, next_gcol=None, hook=None):
        for (h0, hn) in ((0, 16), (16, 16), (32, 12)):
            for j in range(hn // 2):
                n0 = (h0 + 2 * j) * 128
                s1 = load_w(w1v[f][:, :, n0:n0 + 256], 16)
                s3 = load_w(w3v[f][:, :, n0:n0 + 256], 16)
                for m in range(2):
                    gc = 2 * j + m
                    for (t0, tn) in tiles:
                        bA = nb()
                        mm_group(bA, tn, s1, m, 16, lambda k: HN[:, k, t0:t0 + tn], lambda k: T("hn", k, t0, t0 + tn),
                                 stat_tile=(t0, tn))
                        bB = nb()
                        mm_group(bB, tn, s3, m, 16, lambda k: HN[:, k, t0:t0 + tn], lambda k: T("hn", k, t0, t0 + tn))
                        q = sq_i[0] % 2
                        sq_i[0] += 1
                        rt = T("rstd", 0, t0, t0 + tn)
                        S.add("dve", CALL("tensor_tensor", out=SQ[q][:, 0:tn], in0=ps[bA][:, 0:tn], in1=RSTD[:, t0:t0 + tn], op=ALU.mult),
                              reads=[("ps", bA)] + rt, writes=[("sq", q)])
                        S.add("act", CALL("activation", out=SQ[q][:, 0:tn], in_=SQ[q][:, 0:tn], func=AF.Silu),
                              reads=[("sq", q)], writes=[("sq", q)])
                        S.add("dve", CALL("tensor_tensor", out=SQ[q][:, 0:tn], in0=SQ[q][:, 0:tn], in1=RSTD[:, t0:t0 + tn], op=ALU.mult),
                              reads=[("sq", q)] + rt, writes=[("sq", q)])
                        S.add("dve", CALL("tensor_tensor",
                            out=G[:, gc, t0:t0 + tn], in0=SQ[q][:, 0:tn], in1=ps[bB][:, 0:tn], op=ALU.mult),
                            reads=[("sq", q), ("ps", bB)], writes=T("g", gc, t0, t0 + tn))
            if h0 == 0 and hook is not None:
                hook()
            if h0 == 32:
                zero_rstd(tiles[0][0], W)
            for nb2 in range(8):
                s = load_w(w2v[f][:, h0:h0 + hn, nb2 * 256:(nb2 + 1) * 256], hn)
                for m in range(2):
                    oc = nb2 * 2 + m
                    for (t0, tn) in tiles:
                        b = nb()
                        mm_group(b, tn, s, m, hn, lambda k: G[:, k, t0:t0 + tn], lambda k: T("g", k, t0, t0 + tn))
                        S.add("dve", CALL("scalar_tensor_tensor",
                            out=X[:, oc, t0:t0 + tn], in0=ps[b][:, 0:tn], scalar=0.5, in1=X[:, oc, t0:t0 + tn],
                            op0=ALU.mult, op1=ALU.add),
                            reads=[("ps", b)] + T("x", oc, t0, t0 + tn), writes=T("x", oc, t0, t0 + tn))
                        if h0 == 32:
                            if next_gcol is not None:
                                hn_xg(oc, t0, tn, next_gcol)
                            accum_sq(X[:, oc, t0:t0 + tn], T("x", oc, t0, t0 + tn), t0, tn)
            if h0 == 32:
                flush_all()

    def mu_toks(t0, tn):
        return T(LNC[0]["name"], 0, t0, t0 + tn)

    def rs_toks(t0, tn):
        return T(LNC[0]["name"], 1, t0, t0 + tn)

    def ln_zero():
        S.add("dve", CALL("memset", LNC[0]["mu"], 0.0), writes=T(LNC[0]["name"], 0, 0, 536))
        S.add("dve", CALL("memset", LNC[0]["rs"], 0.0), writes=T(LNC[0]["name"], 1, 0, 536))

    def ln_flush():
        pass

    def ln_accum(src, src_toks, t0, tn):
        MU, RS = LNC[0]["mu"], LNC[0]["rs"]
        S.add("dve", CALL("tensor_tensor", out=MU[:, t0:t0 + tn], in0=MU[:, t0:t0 + tn], in1=src, op=ALU.add),
              reads=src_toks + mu_toks(t0, tn), writes=mu_toks(t0, tn))
        q = sq_i[0] % 2
        sq_i[0] += 1
        S.add("act", CALL("activation", out=SQ[q][:, 0:tn], in_=src, func=AF.Square), reads=src_toks, writes=[("sq", q)])
        S.add("dve", CALL("tensor_tensor", out=RS[:, t0:t0 + tn], in0=RS[:, t0:t0 + tn], in1=SQ[q][:, 0:tn], op=ALU.add),
              reads=[("sq", q)] + rs_toks(t0, tn), writes=rs_toks(t0, tn))

    def ln_stats(nch, tiles):
        inv = 1.0 / (nch * 128)
        MU, RS = LNC[0]["mu"], LNC[0]["rs"]
        for (t0, tn) in tiles:
            b1 = nb()
            S.add("pe", CALL("matmul", ps[b1][:, 0:tn], lhsT=ONES, rhs=MU[:, t0:t0 + tn], start=True, stop=True),
                  reads=["ones"] + mu_toks(t0, tn), writes=[("ps", b1)])
            b2 = nb()
            S.add("pe", CALL("matmul", ps[b2][:, 0:tn], lhsT=ONES, rhs=RS[:, t0:t0 + tn], start=True, stop=True),
                  reads=["ones"] + rs_toks(t0, tn), writes=[("ps", b2)])
            S.add("dve", CALL("tensor_scalar", out=MU[:, t0:t0 + tn], in0=ps[b1][:, 0:tn], scalar1=inv, scalar2=None, op0=ALU.mult),
                  reads=[("ps", b1)], writes=mu_toks(t0, tn))
            S.add("dve", CALL("tensor_tensor", out=SQ[0][:, 0:tn], in0=MU[:, t0:t0 + tn], in1=MU[:, t0:t0 + tn], op=ALU.mult),
                  reads=mu_toks(t0, tn), writes=[("sq", 0)])
            S.add("dve", CALL("scalar_tensor_tensor", out=RS[:, t0:t0 + tn], in0=ps[b2][:, 0:tn], scalar=inv, in1=SQ[0][:, 0:tn],
                              op0=ALU.mult, op1=ALU.subtract),
                  reads=[("ps", b2), ("sq", 0)], writes=rs_toks(t0, tn))
            rstd_from(RS[:, t0:t0 + tn], rs_toks(t0, tn), RS[:, t0:t0 + tn], rs_toks(t0, tn), 1.0)

    def ln_apply(buf, name, c, tiles, finish):
        MU, RS = LNC[0]["mu"], LNC[0]["rs"]
        for (t0, tn) in tiles:
            q = sq_i[0] % 2
            sq_i[0] += 1
            S.add("dve", CALL("tensor_tensor", out=SQ[q][:, 0:tn], in0=buf[:, c, t0:t0 + tn], in1=MU[:, t0:t0 + tn], op=ALU.subtract),
                  reads=T(name, c, t0, t0 + tn) + mu_toks(t0, tn), writes=[("sq", q)])
            S.add("dve", CALL("tensor_tensor", out=SQ[q][:, 0:tn], in0=SQ[q][:, 0:tn], in1=RS[:, t0:t0 + tn], op=ALU.mult),
                  reads=[("sq", q)] + rs_toks(t0, tn), writes=[("sq", q)])
            finish(c, t0, tn, SQ[q][:, 0:tn], ("sq", q))

    def layernorm_cols(buf, name, nch, tiles, finish):
        ln_stats(nch, tiles)
        for c in range(nch):
            ln_apply(buf, name, c, tiles, finish)

    def mixer0():
        LNC[0] = LNB
        cur_r = [o_R]

        def ralloc(n):
            o = cur_r[0]
            cur_r[0] += (n + 7) // 8 * 8
            assert cur_r[0] <= AW, "R overflow mixer0"
            return o

        AC = vf32(ralloc(8 * 536), 8 * 536).rearrange("p (c t) -> p c t", t=536)
        AO = vbf(ralloc(4 * 536), 4 * 536).rearrange("p (c t) -> p c t", t=536)
        BO = vbf(ralloc(4 * 536), 4 * 536).rearrange("p (c t) -> p c t", t=536)
        o_A = cur_r[0]
        AFB = vf32(ralloc(568), 568)
        AFB16s = [vbf(ralloc(284), 284) for _ in range(2)]
        DG = vbf(ralloc(31 * 64), 31 * 64).rearrange("p (k m) -> p k m", m=128)
        FULLS16s = [vbf(ralloc(248), 248).rearrange("p (k b) -> p k b", b=16) for _ in range(2)]
        cur_r[0] = o_A
        CHF = vf32(ralloc(568), 568)
        BG = vf32(ralloc(536), 536)
        ACCB = vf32(ralloc(536), 536)

        for half in range(2):
            g0 = 536 * half
            tl_in = [(0, 268), (268, 268)]
            if half == 0:
                tl_out = [(32, 252), (284, 252)]
                p0, pn = 32, 504
            else:
                tl_out = [(0, 268), (268, 268)]
                p0, pn = 0, 520
            ln_zero()
            if half == 0:
                S.add("dve", CALL("memset", AFB[:, 0:32], 0.0), writes=[("afb", 0)])
            slots_a = {}

            def A1(c):
                cb, m = c // 2, c % 2
                if m == 0:
                    slots_a[cb] = (load_w(abinv[:, :, cb * 256:(cb + 1) * 256], 16),
                                   load_w(abinv[:, :, 1024 + cb * 256:1024 + (cb + 1) * 256], 16))
                s_pa, s_ga = slots_a[cb]
                AFB16 = AFB16s[c % 2]
                FULLS16 = FULLS16s[c % 2]
                if half == 1:
                    S.add("act", CALL("copy", out=AFB[:, 0:32], in_=ACARRY[:, c, :]),
                          reads=[("acarry", c)], writes=[("afb", 0)])
                for (t0, tn) in tl_in:
                    gt = g0 + t0
                    b1 = nb()
                    mm_group(b1, tn, s_pa, m, 16, lambda k: HN[:, k, gt:gt + tn], lambda k: T("hn", k, gt, gt + tn))
                    b2 = nb()
                    mm_group(b2, tn, s_ga, m, 16, lambda k: HN[:, k, gt:gt + tn], lambda k: T("hn", k, gt, gt + tn))
                    tq = 0 if t0 == 0 else 1
                    rt = T("rstd", 0, gt, gt + tn)
                    S.add("dve", CALL("tensor_tensor", out=SQ[tq][:, 0:tn], in0=ps[b2][:, 0:tn], in1=RSTD[:, gt:gt + tn], op=ALU.mult),
                          reads=[("ps", b2)] + rt, writes=[("sq", tq)])
                    S.add("act", CALL("activation", out=SQ[tq][:, 0:tn], in_=SQ[tq][:, 0:tn], func=AF.Sigmoid),
                          reads=[("sq", tq)], writes=[("sq", tq)])
                    S.add("dve", CALL("tensor_tensor", out=SQ[tq][:, 0:tn], in0=SQ[tq][:, 0:tn], in1=RSTD[:, gt:gt + tn], op=ALU.mult),
                          reads=[("sq", tq)] + rt, writes=[("sq", tq)])
                    S.add("dve", CALL("tensor_tensor", out=AFB[:, 32 + t0:32 + t0 + tn], in0=SQ[tq][:, 0:tn],
                                      in1=ps[b1][:, 0:tn], op=ALU.mult),
                          reads=[("ps", b1), ("sq", tq)], writes=[("afb", 1 + t0)])
                afb_all = [("afb", 0), ("afb", 1), ("afb", 269)]
                if half == 0:
                    S.add("act", CALL("copy", out=ACARRY[:, c, :], in_=AFB[:, 536:568]),
                          reads=afb_all, writes=[("acarry", c)])
                S.add("act", CALL("copy", out=AFB16, in_=AFB), reads=afb_all, writes=[("afb16", c % 2)])
                if half == 1:
                    S.add("sp", CALL("dma_start", out=caP[c * 128:(c + 1) * 128, :], in_=AFB[:, 32 + 490:32 + 520]),
                          reads=afb_all, writes=[("caP", c)], slot=("caP", c % 2))
                    S.add("sp", CALL("dma_start", out=FULLS[:, 0:30, :], in_=sa[c * 128:(c + 1) * 128, :, :]),
                          writes=[("fulls", 0)], slot=("fulls", 0))
                    S.add("act", CALL("copy", out=FULLS[:, 30, :], in_=AFB[:, 32 + 520:32 + 536]),
                          reads=afb_all, writes=[("fulls", 1)])
                    S.add("act", CALL("copy", out=FULLS16, in_=FULLS), reads=[("fulls", 0), ("fulls", 1)], writes=[("fulls16", c % 2)])
                    S.add("sp", CALL("dma_start", out=caS[c * 128:(c + 1) * 128, :, :], in_=FULLS[:, 1:31, :]),
                          reads=[("fulls", 0), ("fulls", 1)], writes=[("caS", c)], slot=("caS", c % 2))

            def DGB(c):
                for k in range(31):
                    if k % 2 == 0:
                        S.add("dve", CALL("tensor_scalar", out=DG[:, k, :], in0=IDF, scalar1=prm_col(P_ACW + c * 31 + k), scalar2=None,
                                          op0=ALU.mult), reads=["idf", "prm"], writes=[("dg", k)])
                    else:
                        S.add("act", CALL("mul", out=DG[:, k, :], in_=IDF, mul=prm_col(P_ACW + c * 31 + k)),
                              reads=["idf", "prm"], writes=[("dg", k)])

            def A2(c):
                AFB16 = AFB16s[c % 2]
                FULLS16 = FULLS16s[c % 2]
                for ti, (t0, tn) in enumerate(tl_out):
                    bank = nb()
                    pn_t = tn
                    if half == 1 and ti == 1:
                        pn_t = tn - 16
                    for k in range(31):
                        S.add("pe", CALL("matmul", ps[bank][:, 0:pn_t], lhsT=DG[:, k, :], rhs=AFB16[:, 2 + t0 + k:2 + t0 + k + pn_t],
                                         start=(k == 0), stop=(k == 30), skip_group_check=True),
                              reads=[("dg", k), ("afb16", c % 2)], writes=[("ps", bank)])
                    if pn_t != tn:
                        for k in range(31):
                            S.add("pe", CALL("matmul", ps[bank][:, pn_t:tn], lhsT=DG[:, k, :], rhs=FULLS16[:, k, :],
                                             start=(k == 0), stop=(k == 30), skip_group_check=True),
                                  reads=[("dg", k), ("fulls16", c % 2)], writes=[("ps", bank)])
                    S.add("act", CALL("activation", out=AC[:, c, t0:t0 + tn], in_=ps[bank][:, 0:tn], func=AF.Identity,
                                      bias=prm_col(P_ACB + c), scale=1.0),
                          reads=[("ps", bank), "prm"], writes=T("ac", c, t0, t0 + tn))
                    ln_accum(AC[:, c, t0:t0 + tn], T("ac", c, t0, t0 + tn), t0, tn)

            A1(0)
            DGB(0)
            for c in range(8):
                if c + 1 < 8:
                    A1(c + 1)
                A2(c)
                if c + 1 < 8:
                    DGB(c + 1)
            def fin_a(c, t0, tn, tmp, tmptok):
                S.add("act", CALL("activation",
                    out=AO[:, c, t0:t0 + tn], in_=tmp, func=AF.Silu, bias=prm_col(P_ALB + c), scale=prm_col(P_ALG + c)),
                    reads=[tmptok, "prm"], writes=T("ao", c, t0, t0 + tn))
            S.barrier(dma_slots=[("caP", 0), ("caP", 1), ("cbP", 0), ("cbP", 1)])
            ln_stats(8, tl_out)
            for cb in range(4):
                s_bg = load_w(abinv[:, :, 2048 + cb * 256:2048 + (cb + 1) * 256], 16)
                s_cg = load_w(abinv[:, :, 3072 + cb * 256:3072 + (cb + 1) * 256], 16)
                s_hb = load_w(abinv[:, :, 4096 + cb * 256:4096 + (cb + 1) * 256], 16)
                for m in range(2):
                    c = cb * 2 + m
                    if half == 1:
                        S.add("act", CALL("copy", out=CHF[:, 0:32], in_=CHCARRY[:, c, :]),
                              reads=[("chcarry", c)], writes=[("chf", 0)])
                    for (t0, tn) in tl_in:
                        gt = g0 + t0
                        b1 = nb()
                        mm_group(b1, tn, s_bg, m, 16, lambda k: HN[:, k, gt:gt + tn], lambda k: T("hn", k, gt, gt + tn))
                        b2 = nb()
                        mm_group(b2, tn, s_cg, m, 16, lambda k: HN[:, k, gt:gt + tn], lambda k: T("hn", k, gt, gt + tn))
                        b3 = nb()
                        mm_group(b3, tn, s_hb, m, 16, lambda k: HN[:, k, gt:gt + tn], lambda k: T("hn", k, gt, gt + tn))
                        rt = T("rstd", 0, gt, gt + tn)
                        S.add("dve", CALL("tensor_tensor", out=BG[:, t0:t0 + tn], in0=ps[b1][:, 0:tn], in1=RSTD[:, gt:gt + tn], op=ALU.mult),
                              reads=[("ps", b1)] + rt, writes=[("bg", t0)])
                        tq = 0 if t0 == 0 else 1
                        S.add("dve", CALL("tensor_tensor", out=SQ[tq][:, 0:tn], in0=ps[b3][:, 0:tn], in1=RSTD[:, gt:gt + tn], op=ALU.mult),
                              reads=[("ps", b3)] + rt, writes=[("sq", tq)])
                        S.add("dve", CALL("tensor_tensor", out=CHF[:, 32 + t0:32 + t0 + tn], in0=ps[b2][:, 0:tn], in1=RSTD[:, gt:gt + tn],
                                          op=ALU.mult),
                              reads=[("ps", b2)] + rt, writes=[("chf", 1 + t0)])
                        S.add("dve", CALL("tensor_tensor", out=CHF[:, 32 + t0:32 + t0 + tn], in0=CHF[:, 32 + t0:32 + t0 + tn],
                                          in1=SQ[tq][:, 0:tn], op=ALU.mult),
                              reads=[("sq", tq), ("chf", 1 + t0)], writes=[("chf", 1 + t0)])
                    chf_all = [("chf", 0), ("chf", 1), ("chf", 269)]
                    if half == 0:
                        S.add("act", CALL("copy", out=CHCARRY[:, c, :], in_=CHF[:, 536:568]),
                              reads=chf_all, writes=[("chcarry", c)])
                    acc = ACCB[:, p0:p0 + pn]
                    for k in range(3):
                        src = CHF[:, 30 + p0 + k:30 + p0 + k + pn]
                        wk = prm_col(P_BCW + c * 3 + k)
                        if k == 0:
                            S.add("dve", CALL("tensor_scalar", out=acc, in0=src, scalar1=wk, scalar2=None, op0=ALU.mult),
                                  reads=chf_all + ["prm"], writes=[("accb", 0)])
                        else:
                            S.add("dve", CALL("scalar_tensor_tensor",
                                out=acc, in0=src, scalar=wk, in1=acc, op0=ALU.mult, op1=ALU.add),
                                reads=chf_all + [("accb", 0), "prm"], writes=[("accb", 0)])
                    bread = [("accb", 0)]
                    if half == 1:
                        S.add("sp", CALL("dma_start", out=cbP[c * 128:(c + 1) * 128, :], in_=CHF[:, 32 + 518:32 + 520]),
                              reads=chf_all, writes=[("cbP", c)], slot=("cbP", c % 2))
                        S.add("sp", CALL("dma_start", out=FULLSB[:, 0:2, :], in_=sb[c * 128:(c + 1) * 128, :, :]),
                              writes=[("fullsb", 0)], slot=("fullsb", 0))
                        S.add("act", CALL("copy", out=FULLSB[:, 2, :], in_=CHF[:, 32 + 520:32 + 536]),
                              reads=chf_all, writes=[("fullsb", 1)])
                        accs = ACCB[:, 520:536]
                        for k in range(3):
                            wk = prm_col(P_BCW + c * 3 + k)
                            src = FULLSB[:, k, :]
                            if k == 0:
                                S.add("dve", CALL("tensor_scalar", out=accs, in0=src, scalar1=wk, scalar2=None, op0=ALU.mult),
                                      reads=[("fullsb", 0), ("fullsb", 1), "prm"], writes=[("accb", 1)])
                            else:
                                S.add("dve", CALL("scalar_tensor_tensor",
                                    out=accs, in0=src, scalar=wk, in1=accs, op0=ALU.mult, op1=ALU.add),
                                    reads=[("fullsb", 0), ("fullsb", 1), ("accb", 1), "prm"], writes=[("accb", 1)])
                        S.add("sp", CALL("dma_start", out=cbS[c * 128:(c + 1) * 128, :, :], in_=FULLSB[:, 1:3, :]),
                              reads=[("fullsb", 0), ("fullsb", 1)], writes=[("cbS", c)], slot=("cbS", c % 2))
                        bread = [("accb", 0), ("accb", 1)]
                    o0 = tl_out[0][0]
                    on = tl_out[-1][0] + tl_out[-1][1] - o0
                    S.add("dve", CALL("tensor_tensor", out=BO[:, c, o0:o0 + on], in0=ACCB[:, o0:o0 + on],
                                                                              in1=BG[:, o0:o0 + on], op=ALU.mult),
                          reads=bread + [("bg", 0), ("bg", 268)], writes=T("bo", c, o0, o0 + on))
                    ln_apply(AC, "ac", c, tl_out, fin_a)
            if half == 1:
                half0_sq(32, 536)
                zero_rstd(536, W)
            for nb2 in range(8):
                s = load_w(aboutv[:, :, nb2 * 256:(nb2 + 1) * 256], 16)
                for m in range(2):
                    oc = nb2 * 2 + m
                    for (t0, tn) in tl_out:
                        b = nb()
                        mm_group(b, tn, s, m, 16,
                                 lambda k: (AO[:, k, t0:t0 + tn] if k < 8 else BO[:, k - 8, t0:t0 + tn]),
                                 lambda k: (T("ao", k, t0, t0 + tn) if k < 8 else T("bo", k - 8, t0, t0 + tn)))
                        gt = g0 + t0
                        S.add("dve", CALL("tensor_tensor",
                            out=X[:, oc, gt:gt + tn], in0=ps[b][:, 0:tn], in1=X[:, oc, gt:gt + tn], op=ALU.add),
                            reads=[("ps", b)] + T("x", oc, gt, gt + tn), writes=T("x", oc, gt, gt + tn))
                        post_mixer_epilogue(2, half, oc, gt, tn)
            if half == 1:
                flush_all()
            S.barrier(dma_slots=[("caP", 0), ("caP", 1), ("cbP", 0), ("cbP", 1)])
        S.add("sp", CALL("dma_start", out=BROW, in_=brow.partition_broadcast(128)),
              writes=["brow"] + T("lnb0", 0, 0, 536) + T("lnb0", 1, 0, 536), slot="brow")

    def mixer1():
        LNC[0] = LNB1
        S.barrier(dma_slots=[("caS", 0), ("caS", 1), ("cbS", 0), ("cbS", 1), ("fulls", 0), ("fullsb", 0)])
        o_V = o_R
        V = vf32(o_V, 16 * 528).rearrange("p (c t) -> p c t", t=528)
        o_VNB = o_R + 16 * 528
        VNB = vbf(o_VNB, 8 * 528).rearrange("p (c t) -> p c t", t=528)
        assert o_VNB + 8 * 528 <= AW, "R overflow mixer1"
        VT = vbf(o_V, 5 * 1024).rearrange("p (n d) -> p n d", d=2048)
        UT = vf32(o_V + 5120, 528)
        T2 = vf32(o_V + 5120 + 528, 528)
        PT = [ps[6][:, :].bitcast(BF16), ps[7][:, :].bitcast(BF16)]
        for half in range(2):
            if half == 0:
                g0 = 32
                hw = 512
                tiles = [(0, 512)]
                chunks = [(i * 128, 128) for i in range(4)]
                sgroups = [(0, 512, [0, 1, 2, 3])]
            else:
                g0 = 544
                hw = 528
                tiles = [(0, 264), (264, 264)]
                chunks = [(i * 128, 128) for i in range(4)] + [(512, 16)]
                sgroups = [(0, 512, [0, 1, 2, 3]), (512, 16, [4])]
            ln_zero()
            for nb2 in range(8):
                s = load_w(cinv[:, :, 2048 + nb2 * 256:2048 + (nb2 + 1) * 256], 16)
                for m in range(2):
                    oc = nb2 * 2 + m
                    for (t0, tn) in tiles:
                        gt = g0 + t0
                        b = nb()
                        mm_group(b, tn, s, m, 16, lambda k: HN[:, k, gt:gt + tn], lambda k: T("hn", k, gt, gt + tn))
                        q = sq_i[0] % 2
                        sq_i[0] += 1
                        S.add("dve", CALL("tensor_tensor", out=SQ[q][:, 0:tn], in0=ps[b][:, 0:tn], in1=RSTD[:, gt:gt + tn], op=ALU.mult),
                              reads=[("ps", b)] + T("rstd", 0, gt, gt + tn), writes=[("sq", q)])
                        S.add("act", CALL("activation",
                            out=V[:, oc, t0:t0 + tn], in_=SQ[q][:, 0:tn], func=AF.Gelu, bias=prm_col(P_CBI + 16 + oc), scale=1.0),
                            reads=[("sq", q), "prm"], writes=T("v", oc, t0, t0 + tn))
                        ln_accum(V[:, oc, t0:t0 + tn], T("v", oc, t0, t0 + tn), t0, tn)
            def fin_v(c, t0, tn, tmp, tmptok):
                S.add("act", CALL("activation",
                    out=VNB[:, c, t0:t0 + tn], in_=tmp, func=AF.Identity, bias=prm_col(P_CLB + c), scale=prm_col(P_CLG + c)),
                    reads=[tmptok, "prm"], writes=T("vnb", c, t0, t0 + tn))
                if t0 + tn == 528:
                    S.add("act", CALL("activation",
                        out=VS[:, c, :], in_=tmp[:, tn - 16:tn], func=AF.Identity, bias=prm_col(P_CLB + c), scale=prm_col(P_CLG + c)),
                        reads=[tmptok, "prm"], writes=[("vs", c)])
            layernorm_cols(V, "v", 16, tiles, fin_v)
            if half == 1:
                S.add("sp", CALL("dma_start", out=vSv, in_=VS), reads=[("vs", c) for c in range(16)], writes=["vS"], slot="vS")
            S.barrier()
            ti = 0
            for n, (c0, cw) in enumerate(chunks):
                for q4 in range(4):
                    pb = ti % 2
                    ti += 1
                    for qq in range(4):
                        oc = q4 * 4 + qq
                        S.add("pe", CALL("transpose",
                            out=PT[pb][0:cw, qq * 128:(qq + 1) * 128], in_=VNB[:, oc, c0:c0 + cw], identity=IDB),
                            reads=T("vnb", oc, c0, c0 + cw) + ["idb"], writes=[("ps", 6 + pb)])
                    if pb == 0:
                        S.add("act", CALL("copy", out=VT[0:cw, n, q4 * 512:(q4 + 1) * 512], in_=PT[pb][0:cw, 0:512]),
                              reads=[("ps", 6 + pb)], writes=[("vt", n, q4)])
                    else:
                        S.add("dve", CALL("tensor_copy", out=VT[0:cw, n, q4 * 512:(q4 + 1) * 512], in_=PT[pb][0:cw, 0:512]),
                              reads=[("ps", 6 + pb)], writes=[("vt", n, q4)])
            US = VNB
            for nb2 in range(8):
                s = load_w(cinv[:, :, nb2 * 256:(nb2 + 1) * 256], 16)
                for m in range(2):
                    oc = nb2 * 2 + m
                    h = oc // 2
                    for (t0, tn) in tiles:
                        gt = g0 + t0
                        b = nb()
                        mm_group(b, tn, s, m, 16, lambda k: HN[:, k, gt:gt + tn], lambda k: T("hn", k, gt, gt + tn))
                        q = sq_i[0] % 2
                        sq_i[0] += 1
                        S.add("dve", CALL("tensor_tensor", out=SQ[q][:, 0:tn], in0=ps[b][:, 0:tn], in1=RSTD[:, gt:gt + tn], op=ALU.mult),
                              reads=[("ps", b)] + T("rstd", 0, gt, gt + tn), writes=[("sq", q)])
                        S.add("act", CALL("activation",
                            out=UT[:, t0:t0 + tn], in_=SQ[q][:, 0:tn], func=AF.Gelu, bias=prm_col(P_CBI + oc), scale=1.0),
                            reads=[("sq", q), "prm"], writes=[("ut", t0)])
                    t2toks = []
                    for (s0, sn, cl) in sgroups:
                        b2 = nb()
                        first = True
                        for n in cl:
                            c0, cw = chunks[n]
                            lo = c0 - s0
                            rhs = WSTB[0:cw, h, 0:cw] if cw == 128 else WSSB[0:cw, h, 0:cw]
                            rtok = ["wstb"] if cw == 128 else [("wssb", h)]
                            S.add("pe", CALL("matmul",
                                ps[b2][:, lo:lo + cw], lhsT=VT[0:cw, n, oc * 128:(oc + 1) * 128], rhs=rhs, start=first, stop=True,
                                skip_group_check=True),
                                reads=[("vt", n, oc // 4)] + rtok, writes=[("ps", b2)])
                            first = False
                        if sn == 16:
                            S.add("dve", CALL("tensor_tensor",
                                out=T2[:, s0:s0 + sn], in0=ps[b2][:, 0:sn], in1=BSBS[:, h, :], op=ALU.add),
                                reads=[("ps", b2), ("bsbs", h)], writes=[("t2", s0)])
                        else:
                            bsrc = BROW[:, B_BS + h * 128:B_BS + (h + 1) * 128]
                            bb = bass.AP(bsrc.tensor, bsrc.offset, [list(bsrc.ap[0]), [0, 4], list(bsrc.ap[1])])
                            S.add("dve", CALL("tensor_tensor",
                                out=T2[:, s0:s0 + sn].rearrange("p (n t) -> p n t", t=128),
                                in0=ps[b2][:, 0:sn].rearrange("p (n t) -> p n t", t=128), in1=bb, op=ALU.add),
                                reads=[("ps", b2), "brow"], writes=[("t2", s0)])
                        t2toks.append(("t2", s0))
                    S.add("dve", CALL("tensor_tensor",
                        out=US[:, oc, 0:hw], in0=T2[:, 0:hw], in1=UT[:, 0:hw], op=ALU.mult),
                        reads=t2toks + [("ut", t0) for (t0, _) in tiles], writes=T("vnb", oc, 0, hw))
            if half == 1:
                half0_sq(32, 544)
                zero_rstd(544, W)
            for nb2 in range(8):
                s = load_w(coutv[:, :, nb2 * 256:(nb2 + 1) * 256], 16)
                for m in range(2):
                    oc = nb2 * 2 + m
                    for (t0, tn) in tiles:
                        b = nb()
                        mm_group(b, tn, s, m, 16, lambda k: US[:, k, t0:t0 + tn], lambda k: T("vnb", k, t0, t0 + tn))
                        gt = g0 + t0
                        S.add("dve", CALL("tensor_tensor",
                            out=X[:, oc, gt:gt + tn], in0=ps[b][:, 0:tn], in1=X[:, oc, gt:gt + tn], op=ALU.add),
                            reads=[("ps", b)] + T("x", oc, gt, gt + tn), writes=T("x", oc, gt, gt + tn))
                        post_mixer_epilogue(5, half, oc, gt, tn)
            if half == 1:
                flush_all()
            S.barrier()

    TL_ALL = [(0, 358), (358, 358), (716, 356)]
    TL_MAIN = [(32, 348), (380, 346), (726, 346)]

    def dump_x():
        for k in range(16):
            S.add("sp", CALL("dma_start", out=yTv[:, k, :], in_=X[:, k, 32:W]), reads=T("x", k, 32, W),
                  writes=[("y", k)], slot=("y", k % 4))

    def half0_sq(c_lo, c_hi):
        zero_rstd(c_lo, c_hi)
        for k in range(16):
            accum_sq(X[:, k, c_lo:c_hi], T("x", k, c_lo, c_hi), c_lo, c_hi - c_lo)
        flush_all()

    def run():
        zero_rstd(0, W)
        for (t0, tn) in TL_ALL:
            for k in range(16):
                if k % 2 == 0:
                    hn_xg(k, t0, tn, P_NG + 0 * 16)
                else:
                    S.add("dve", CALL("tensor_scalar", out=HN[:, k, t0:t0 + tn], in0=X[:, k, t0:t0 + tn], scalar1=prm_col(P_NG + k),
                                      scalar2=None, op0=ALU.mult),
                          reads=T("x", k, t0, t0 + tn) + ["prm"], writes=T("hn", k, t0, t0 + tn))
            for k in range(16):
                accum_sq(X[:, k, t0:t0 + tn], T("x", k, t0, t0 + tn), t0, tn)
        flush_all()
        rstd_defer(TL_ALL, per_tile=True)
        ffn(0, TL_ALL, next_gcol=P_NG + 1 * 16, hook=setup_spatial)
        S.barrier()
        if stop == "f0":
            return dump_x()
        rstd_defer(TL_ALL)
        mixer0()
        if stop == "m0":
            return dump_x()
        rstd_defer(TL_MAIN)
        ffn(1, TL_MAIN, next_gcol=P_NG + 3 * 16)
        if stop == "f1":
            return dump_x()
        rstd_defer(TL_MAIN)
        ffn(2, TL_MAIN, next_gcol=P_NG + 4 * 16)
        rstd_defer(TL_MAIN)
        mixer1()
        if stop == "m1":
            return dump_x()
        rstd_defer(TL_MAIN)
        ffn(3, TL_MAIN)
        S.barrier()
        YS = vf32(o_R, 12 * 512).rearrange("p (q t) -> p q t", t=512)
        cnt = [0]

        for (t0, tn) in TL_MAIN:
            bank = nb()
            S.add("pe", CALL("matmul", ps[bank][:, 0:tn], lhsT=ONES, rhs=RSTD[:, t0:t0 + tn], start=True, stop=True),
                  reads=["ones"] + T("rstd", 0, t0, t0 + tn), writes=[("ps", bank)])
            rstd_from(ps[bank][:, 0:tn], [("ps", bank)], RSTD[:, t0:t0 + tn], T("rstd", 0, t0, t0 + tn), 1.0 / D)
            for k in range(16):
                q = cnt[0] % 12
                cnt[0] += 1
                if k % 3 == 2:
                    S.add("act", CALL("mul", out=YS[:, q, 0:tn], in_=X[:, k, t0:t0 + tn], mul=prm_col(P_FG + k)),
                          reads=T("x", k, t0, t0 + tn) + ["prm"], writes=[("ys", q)])
                    S.add("pool", CALL("tensor_tensor", out=YS[:, q, 0:tn], in0=YS[:, q, 0:tn], in1=RSTD[:, t0:t0 + tn], op=ALU.mult),
                          reads=[("ys", q)] + T("rstd", 0, t0, t0 + tn), writes=[("ys", q)])
                else:
                    S.add("dve", CALL("scalar_tensor_tensor",
                        out=YS[:, q, 0:tn], in0=X[:, k, t0:t0 + tn], scalar=prm_col(P_FG + k), in1=RSTD[:, t0:t0 + tn],
                        op0=ALU.mult, op1=ALU.mult),
                        reads=T("x", k, t0, t0 + tn) + T("rstd", 0, t0, t0 + tn) + ["prm"], writes=[("ys", q)])
                S.add("sp", CALL("dma_start", out=yTv[:, k, t0 - 32:t0 - 32 + tn], in_=YS[:, q, 0:tn]),
                      reads=[("ys", q)], writes=[("y", k, t0)], slot=("y", q))

    run()
    finals = [k for k in S.slot_cnt if k[0] in ("y", "caP", "cbP", "caS", "cbS") or k == "vS"]
    stats = S.emit(nc, final_slots=finals)
    return nc, stats, S.n


def _pcol(v):
    v = np.asarray(v, np.float32)
    return np.ascontiguousarray(v.reshape(-1, 128).T)


def make_in_maps(inp):
    f32 = np.float32
    xp = np.asarray(inp["x_prompt"], f32)
    xs = np.asarray(inp["x_sample"], f32)
    sa = np.asarray(inp["state_conv_a"], f32)[0]
    sb = np.asarray(inp["state_conv_b"], f32)[0]
    shared = {
        "w1": np.ascontiguousarray(np.asarray(inp["ffn_w1"], f32).reshape(4, D, DFF)),
        "w3": np.ascontiguousarray(np.asarray(inp["ffn_w3"], f32).reshape(4, D, DFF)),
        "w2": np.ascontiguousarray(np.asarray(inp["ffn_w2"], f32).reshape(4, DFF, D)),
        "abin": np.ascontiguousarray(np.asarray(inp["ab_w_in"], f32)[0]),
        "about": np.ascontiguousarray(np.asarray(inp["ab_w_out"], f32)[0]),
        "cin": np.ascontiguousarray(np.asarray(inp["c_w_in"], f32)[0]),
        "cout": np.ascontiguousarray(np.asarray(inp["c_w_out"], f32)[0]),
    }
    prm = np.zeros((128, P_N), f32)
    ng = np.asarray(inp["norm_g"], f32).reshape(6, D)
    for i in range(6):
        prm[:, P_NG + i * 16:P_NG + (i + 1) * 16] = _pcol(ng[i])
    prm[:, P_FG:P_FG + 16] = _pcol(inp["final_g"])
    acw = np.asarray(inp["a_conv_w"], f32)[0]
    prm[:, P_ACW:P_ACW + 248] = acw.T.reshape(8, 128, 31).transpose(1, 0, 2).reshape(128, 248)
    prm[:, P_ACB:P_ACB + 8] = _pcol(np.asarray(inp["a_conv_b"])[0])
    prm[:, P_ALG:P_ALG + 8] = _pcol(np.asarray(inp["a_ln_g"])[0])
    prm[:, P_ALB:P_ALB + 8] = _pcol(np.asarray(inp["a_ln_b"])[0])
    bcw = np.asarray(inp["b_conv_w"], f32)[0]
    prm[:, P_BCW:P_BCW + 24] = bcw.T.reshape(8, 128, 3).transpose(1, 0, 2).reshape(128, 24)
    prm[:, P_CBI:P_CBI + 32] = _pcol(np.asarray(inp["c_b_in"])[0])
    prm[:, P_CLG:P_CLG + 16] = _pcol(np.asarray(inp["c_ln_g"])[0])
    prm[:, P_CLB:P_CLB + 16] = _pcol(np.asarray(inp["c_ln_b"])[0])
    cws = np.asarray(inp["c_w_s"], f32)[0]
    wst = np.ascontiguousarray(cws.transpose(2, 0, 1).reshape(128, 1024))
    cbs = np.asarray(inp["c_b_s"], f32)[0]
    brow = np.zeros((1, B_N), f32)
    brow[0, B_W00:B_W00 + 8] = cws[:, 0, 0]
    brow[0, B_BS0:B_BS0 + 8] = cbs[:, 0]
    brow[0, B_BS:B_BS + 1024] = cbs.reshape(-1)
    shared.update(prm=prm, wst=wst, brow=brow)
    maps = []
    for i in range(NCORES):
        b, half = i // 2, i % 2
        xt = np.zeros((D, W), f32)
        if half == 1:
            xt[:, 0:HALO] = xp[b, NPR - HALO:NPR, :].T
        xt[:, HALO:HALO + NPR] = xp[b, half * NPR:(half + 1) * NPR, :].T
        xt[:, HALO + NPR:W] = xs[i * NSM:(i + 1) * NSM, 0, :].T
        m = dict(shared)
        m["xT"] = xt
        m["sa"] = np.ascontiguousarray(sa[i * NSM:(i + 1) * NSM].transpose(2, 1, 0))
        m["sb"] = np.ascontiguousarray(sb[i * NSM:(i + 1) * NSM].transpose(2, 1, 0))
        maps.append(m)
    return maps


_CACHE = {}


def kernel(**inp):
    if "nc" not in _CACHE:
        _CACHE["nc"] = build_program(DEBUG_STOP)[0]
    nc = _CACHE["nc"]
    maps = make_in_maps(inp)
    res = run_bass_kernel_spmd(nc, maps, core_ids=list(range(NCORES)))
    R = res.results
    f32 = np.float32
    y_prompt = np.zeros((4, 2048, D), f32)
    y_sample = np.zeros((128, 1, D), f32)
    ca_p = np.zeros((1, 4, 30, 1024), f32)
    cb_p = np.zeros((1, 4, 2, 1024), f32)
    ca_s = np.zeros((1, 128, 30, 1024), f32)
    cb_s = np.zeros((1, 128, 2, 1024), f32)
    v_s = np.zeros((1, 128, 1, D), f32)
    for i in range(NCORES):
        b, half = i // 2, i % 2
        yt = np.asarray(R[i]["yT"], f32)
        y_prompt[b, half * NPR:(half + 1) * NPR, :] = yt[:, 0:NPR].T
        y_sample[i * NSM:(i + 1) * NSM, 0, :] = yt[:, NPR:].T
        if half == 1:
            ca_p[0, b] = np.asarray(R[i]["caP"], f32).T
            cb_p[0, b] = np.asarray(R[i]["cbP"], f32).T
        ca_s[0, i * NSM:(i + 1) * NSM] = np.asarray(R[i]["caS"], f32).transpose(2, 1, 0)
        cb_s[0, i * NSM:(i + 1) * NSM] = np.asarray(R[i]["cbS"], f32).transpose(2, 1, 0)
        v_s[0, i * NSM:(i + 1) * NSM, 0, :] = np.asarray(R[i]["vS"], f32).T
    return (y_prompt, y_sample, ca_p, cb_p, ca_s, cb_s, v_s)
```

```python
from contextlib import ExitStack
import numpy as np
import concourse.bass as bass
import concourse.mybir as mybir
from concourse.bass_utils import run_bass_kernel_spmd

F32 = mybir.dt.float32
F32R = mybir.dt.float32r
BF16 = mybir.dt.bfloat16
ALU = mybir.AluOpType
AF = mybir.ActivationFunctionType

ENGS = ("pe", "act", "dve", "pool", "sp")
NCORES = 8
D = 2048
DFF = 5632
W = 1072
HALO = 32
NPR = 1024
NSM = 16
TOUT = NPR + NSM
EPS = 1e-6
DEBUG_STOP = None


def CALL(name, *a, **kw):
    return (name, a, kw)


class Op:
    __slots__ = ("eng", "fn", "deps", "needed", "sig", "dma", "slot", "val")


class Sched:
    def __init__(self):
        self.ops = {e: [] for e in ENGS}
        self.lastw = {}
        self.readers = {}
        self.slot_cnt = {}
        self.last_dma = {}
        self.n = 0

    def add(self, eng, fn, reads=(), writes=(), slot=None, extra_deps=()):
        o = Op()
        o.eng = eng
        o.fn = fn
        o.needed = False
        o.sig = 0
        o.dma = slot is not None
        o.slot = slot
        o.val = 0
        if o.dma:
            c = self.slot_cnt.get(slot, 0) + 1
            self.slot_cnt[slot] = c
            o.val = 16 * c
            self.last_dma[slot] = o
        deps = {}
        lastw = self.lastw
        readers = self.readers
        for t in reads:
            w = lastw.get(t)
            if w is not None:
                deps[id(w)] = w
        for t in writes:
            w = lastw.get(t)
            if w is not None:
                deps[id(w)] = w
            r = readers.get(t)
            if r:
                for x in r.values():
                    deps[id(x)] = x
        for d in extra_deps:
            deps[id(d)] = d
        dl = []
        for d in deps.values():
            if d is o:
                continue
            if (not d.dma) and (not o.dma) and d.eng == "pe" and eng == "pe":
                continue
            d.needed = True
            dl.append(d)
        o.deps = dl
        rkey = ("q", self.n) if o.dma else eng
        for t in reads:
            r = readers.get(t)
            if r is None:
                readers[t] = {rkey: o}
            else:
                r[rkey] = o
        for t in writes:
            lastw[t] = o
            readers[t] = None
        self.ops[eng].append(o)
        self.n += 1
        return o

    def barrier(self, engs=("pe", "act", "dve"), dma_slots=()):
        last = {}
        for e in engs:
            last[e] = None
            for o in reversed(self.ops[e]):
                if o.fn is not None:
                    last[e] = o
                    break
        dd = [self.last_dma[k] for k in dma_slots if k in self.last_dma]
        for e in engs:
            ds = [last[x] for x in engs if x != e and last[x] is not None]
            self.add(e, None, extra_deps=ds + dd)

    def emit(self, nc, final_slots=()):
        for e in ENGS:
            k = 0
            for o in self.ops[e]:
                if o.needed and not o.dma:
                    k += 1
                    o.sig = k
        stats = {"waits": 0, "ins": 0}
        with ExitStack() as st:
            esem = {e: st.enter_context(nc.semaphore("s_" + e)) for e in ENGS}
            ssem = {}
            for i, k in enumerate(self.slot_cnt):
                ssem[k] = st.enter_context(nc.semaphore("d%d" % i))
            block = st.enter_context(nc.Block())

            def run(ename, eng):
                waited = {}
                for o in self.ops[ename]:
                    need = {}
                    for d in o.deps:
                        if d.dma:
                            key = ("d", d.slot)
                            val = d.val
                        else:
                            key = ("e", d.eng)
                            val = d.sig
                        if need.get(key, 0) < val:
                            need[key] = val
                    for key, val in need.items():
                        if waited.get(key, 0) >= val:
                            continue
                        waited[key] = val
                        sem = ssem[key[1]] if key[0] == "d" else esem[key[1]]
                        eng.wait_ge(sem, val)
                        stats["waits"] += 1
                    if o.fn is None:
                        continue
                    ins = getattr(eng, o.fn[0])(*o.fn[1], **o.fn[2])
                    stats["ins"] += 1
                    if o.dma:
                        ins.then_inc(ssem[o.slot], 16)
                    elif o.needed:
                        ins.then_inc(esem[ename], 1)
                if ename == "sp":
                    for k in final_slots:
                        eng.wait_ge(ssem[k], 16 * self.slot_cnt[k])

            @block.tensor
            def _(e):
                run("pe", e)

            @block.scalar
            def _(e):
                run("act", e)

            @block.vector
            def _(e):
                run("dve", e)

            @block.gpsimd
            def _(e):
                run("pool", e)

            @block.sync
            def _(e):
                run("sp", e)
        return stats


_GB = [0, 32, 252, 264, 268, 284, 348, 358, 380, 512, 520, 528, 536, 544, 568, 716, 726, 788, 800, 804, 808, 820, 1024, 1048, 1056, 1064, 1072]


def gran(a, b):
    return [i for i in range(len(_GB) - 1) if _GB[i] < b and _GB[i + 1] > a]


def T(name, k, a, b):
    return [(name, k, g) for g in gran(a, b)]


P_NG = 0
P_FG = 96
P_ACW = 112
P_ACB = 360
P_ALG = 368
P_ALB = 376
P_BCW = 384
P_CBI = 408
P_CLG = 440
P_CLB = 456
P_N = 472
B_W00 = 0
B_BS0 = 8
B_BS = 16
B_N = 16 + 1024


def build_program(stop=None):
    nc = bass.Bass("TRN2", target_bir_lowering=False)

    def din(name, shape):
        return nc.dram_tensor(name, shape, F32, kind="ExternalInput").ap()

    def dout(name, shape):
        return nc.dram_tensor(name, shape, F32, kind="ExternalOutput").ap()

    xT = din("xT", [D, W])
    sa = din("sa", [1024, 30, NSM])
    sb = din("sb", [1024, 2, NSM])
    w1 = din("w1", [4, D, DFF])
    w3 = din("w3", [4, D, DFF])
    w2 = din("w2", [4, DFF, D])
    abin = din("abin", [D, 5120])
    about = din("about", [D, D])
    cin = din("cin", [D, 4096])
    cout = din("cout", [D, D])
    prm = din("prm", [128, P_N])
    wst = din("wst", [128, 1024])
    brow = din("brow", [1, B_N])
    yT = dout("yT", [D, TOUT])
    caP = dout("caP", [1024, 30])
    cbP = dout("cbP", [1024, 2])
    caS = dout("caS", [1024, 30, NSM])
    cbS = dout("cbS", [1024, 2, NSM])
    vS = dout("vS", [D, NSM])

    def kview(ap2d):
        return ap2d.rearrange("(c p) n -> p c n", p=128)

    w1v = [kview(w1[f]) for f in range(4)]
    w3v = [kview(w3[f]) for f in range(4)]
    w2v = [kview(w2[f]) for f in range(4)]
    abinv = kview(abin)
    aboutv = kview(about)
    cinv = kview(cin)
    coutv = kview(cout)
    xTv = kview(xT)
    yTv = kview(yT)
    vSv = kview(vS)

    S = Sched()
    AW = 53000
    arena = nc.alloc_sbuf_tensor("arena", [128, AW], F32)
    ps = [nc.alloc_psum_tensor("ps%d" % i, [128, 512], F32) for i in range(8)]
    cur = [0]

    def alloc(nwords):
        o = cur[0]
        cur[0] += (nwords + 7) // 8 * 8
        assert cur[0] <= AW, ("arena overflow", cur[0])
        return o

    def vf32(off, n):
        return arena[:, off:off + n]

    def vbf(off, nwords):
        return arena[:, off:off + nwords].bitcast(BF16)

    o_X = alloc(16 * W)
    X = vf32(o_X, 16 * W).rearrange("p (c t) -> p c t", t=W)
    o_HN = alloc(8 * W)
    HN = vbf(o_HN, 8 * W).rearrange("p (c t) -> p c t", t=W)
    WS = []
    for i in range(4):
        o = alloc(2048)
        WS.append(vbf(o, 2048).rearrange("p (c n) -> p c n", n=256))
    PRM = vf32(alloc(P_N), P_N)
    o_BROW = cur[0]
    BROW = vf32(alloc(1072), 1072)[:, 0:B_N]
    ONES = vf32(alloc(128), 128)
    IDF = vf32(alloc(128), 128)
    IDB = vbf(alloc(64), 64)
    RSTD = vf32(alloc(W), W)
    o_SQ = alloc(1024)
    SQ = [vf32(o_SQ, 512), vf32(o_SQ + 512, 512)]
    WSTB = vbf(alloc(512), 512).rearrange("p (h t) -> p h t", t=128)
    WSSB = vbf(alloc(64), 64).rearrange("p (h t) -> p h t", t=16)
    BSBS = vf32(alloc(128), 128).rearrange("p (h t) -> p h t", t=16)
    VS = vf32(alloc(256), 256).rearrange("p (c b) -> p c b", b=16)
    o_FULLS = cur[0]
    FULLS = vf32(alloc(31 * 16), 31 * 16).rearrange("p (k b) -> p k b", b=16)
    FULLSB = vf32(alloc(48), 48).rearrange("p (k b) -> p k b", b=16)
    ACARRY = vf32(alloc(256), 256).rearrange("p (c t) -> p c t", t=32)
    CHCARRY = vf32(alloc(256), 256).rearrange("p (c t) -> p c t", t=32)
    EPSC = vf32(alloc(8), 8)
    prm_eps = EPSC[:, 0:1]
    o_R = cur[0]
    RW = AW - o_R
    LNB = {"mu": vf32(o_BROW, 1072)[:, 0:536], "rs": vf32(o_BROW, 1072)[:, 536:1072], "name": "lnb0"}
    LNB1 = {"mu": vf32(o_FULLS, 1056)[:, 0:528], "rs": vf32(o_FULLS, 1056)[:, 528:1056], "name": "lnb1"}
    LNC = [LNB]

    bank_i = [0]

    def nb():
        b = bank_i[0] % 6
        bank_i[0] += 1
        return b

    def prm_col(c):
        return PRM[:, c:c + 1]

    slot_i = [0]

    def load_w(src_ap, nk):
        sl = slot_i[0] % 4
        dst = WS[sl][:, 0:nk, :]
        xd = [S.last_dma[("x", 2)]] if (slot_i[0] == 2 and ("x", 2) in S.last_dma) else []
        slot_i[0] += 1
        S.add("pool", CALL("dma_start", out=dst, in_=src_ap), writes=[("ws", sl)], slot=("ws", sl), extra_deps=xd)
        return sl

    defer = [None, {}]

    def mm_group(bank, tn, sl, m, nk, rhs_of, rhs_toks, stat_tile=None):
        pst = ps[bank][:, 0:tn]
        for k in range(nk):
            lhsT = WS[sl][:, k, m * 128:(m + 1) * 128]
            rhs = rhs_of(k)
            S.add("pe", CALL("matmul", pst, lhsT=lhsT, rhs=rhs, start=(k == 0), stop=(k == nk - 1)),
                  reads=[("ws", sl)] + rhs_toks(k), writes=[("ps", bank)])
        if defer[0] is not None:
            t = defer[0]
            defer[0] = None
            rstd_begin(t)
        if stat_tile is not None and stat_tile[0] in defer[1]:
            del defer[1][stat_tile[0]]
            rstd_begin([stat_tile])

    def rstd_defer(tiles, per_tile=False):
        if per_tile:
            defer[1] = {t0: tn for (t0, tn) in tiles}
        else:
            defer[0] = tiles

    S.add("sp", CALL("dma_start", out=PRM, in_=prm[:, :]), writes=["prm"], slot="prm")
    S.add("sp", CALL("dma_start", out=BROW, in_=brow.partition_broadcast(128)), writes=["brow"], slot="brow")
    for ti, (t0, tn) in enumerate([(0, 358), (358, 358), (716, 356)]):
        wr = []
        for k in range(16):
            wr += T("x", k, t0, t0 + tn)
        S.add("sp", CALL("dma_start", out=X[:, :, t0:t0 + tn], in_=xTv[:, :, t0:t0 + tn]), writes=wr, slot=("x", ti))
    S.add("dve", CALL("memset", ONES, 1.0), writes=["ones"])
    S.add("dve", CALL("memset", IDF, 1.0), writes=["idf"])
    S.add("pool", CALL("affine_select", out=IDF, in_=IDF, pattern=[[1, 128]], compare_op=ALU.is_equal, fill=0.0,
                                            base=0, channel_multiplier=-1), reads=["idf"], writes=["idf"])
    S.add("dve", CALL("tensor_copy", out=IDB, in_=IDF), reads=["idf"], writes=["idb"])
    def colsum(bank, tn, nk, src_of, src_toks, square):
        for k in range(nk):
            if square:
                sq = SQ[k % 2]
                src = src_of(k)
                sqr = sq[:, 0:tn]
                S.add("act", CALL("activation", out=sqr, in_=src, func=AF.Square),
                      reads=src_toks(k), writes=[("sq", k % 2)])
                S.add("pe", CALL("matmul", ps[bank][:, 0:tn], lhsT=ONES, rhs=sqr, start=(k == 0), stop=(k == nk - 1)),
                      reads=["ones", ("sq", k % 2)], writes=[("ps", bank)])
            else:
                S.add("pe", CALL("matmul", ps[bank][:, 0:tn], lhsT=ONES, rhs=src_of(k), start=(k == 0), stop=(k == nk - 1)),
                      reads=["ones"] + src_toks(k), writes=[("ps", bank)])

    def rstd_from(src, src_toks, dst, dst_toks, inv_n):
        S.add("act", CALL("activation", out=dst, in_=src, func=AF.Sqrt, bias=prm_eps, scale=inv_n),
              reads=src_toks + ["eps"], writes=dst_toks)
        S.add("dve", CALL("reciprocal", out=dst, in_=dst), reads=dst_toks, writes=dst_toks)

    S.add("dve", CALL("memset", EPSC, EPS), writes=["eps"])

    sq_i = [0]
    pend = []

    def zero_rstd(a, b):
        S.add("dve", CALL("memset", RSTD[:, a:b], 0.0), writes=T("rstd", 0, a, b))

    def flush_one():
        q, g0, n = pend.pop(0)
        S.add("dve", CALL("tensor_tensor", out=RSTD[:, g0:g0 + n], in0=RSTD[:, g0:g0 + n], in1=SQ[q][:, 0:n], op=ALU.add),
              reads=[("sq", q)] + T("rstd", 0, g0, g0 + n), writes=T("rstd", 0, g0, g0 + n))

    def flush_all():
        while pend:
            flush_one()

    def accum_sq(src, src_toks, g0, n):
        q = sq_i[0] % 2
        sq_i[0] += 1
        S.add("act", CALL("activation", out=SQ[q][:, 0:n], in_=src, func=AF.Square), reads=src_toks, writes=[("sq", q)])
        pend.append((q, g0, n))
        if len(pend) > 1:
            flush_one()

    def rmsnorm(gcol, tiles, out_of, out_toks, fused):
        if not fused:
            zero_rstd(tiles[0][0], W)
            for (t0, tn) in tiles:
                for k in range(16):
                    accum_sq(X[:, k, t0:t0 + tn], T("x", k, t0, t0 + tn), t0, tn)
            flush_all()
        for (t0, tn) in tiles:
            bank = nb()
            S.add("pe", CALL("matmul", ps[bank][:, 0:tn], lhsT=ONES, rhs=RSTD[:, t0:t0 + tn], start=True, stop=True),
                  reads=["ones"] + T("rstd", 0, t0, t0 + tn), writes=[("ps", bank)])
            rstd_from(ps[bank][:, 0:tn], [("ps", bank)], RSTD[:, t0:t0 + tn], T("rstd", 0, t0, t0 + tn), 1.0 / D)
            for k in range(16):
                o = out_of(k, t0, tn)
                S.add("dve", CALL("scalar_tensor_tensor", out=o, in0=X[:, k, t0:t0 + tn], scalar=prm_col(gcol + k),
                                  in1=RSTD[:, t0:t0 + tn], op0=ALU.mult, op1=ALU.mult),
                      reads=T("x", k, t0, t0 + tn) + T("rstd", 0, t0, t0 + tn) + ["prm"], writes=out_toks(k, t0, tn))

    def norm_to_hn(gcol, tiles, fused):
        rmsnorm(gcol, tiles, lambda k, t0, tn: HN[:, k, t0:t0 + tn], lambda k, t0, tn: T("hn", k, t0, t0 + tn), fused)

    G = vbf(o_R, 8 * W).rearrange("p (c t) -> p c t", t=W)
    assert 8 * W <= RW

    def rstd_begin(tiles):
        for i, (t0, tn) in enumerate(tiles):
            bank = 6 + i % 2
            S.add("pe", CALL("matmul", ps[bank][:, 0:tn], lhsT=ONES, rhs=RSTD[:, t0:t0 + tn], start=True, stop=True),
                  reads=["ones"] + T("rstd", 0, t0, t0 + tn), writes=[("ps", bank)])
            rstd_from(ps[bank][:, 0:tn], [("ps", bank)], RSTD[:, t0:t0 + tn], T("rstd", 0, t0, t0 + tn), 1.0 / D)

    def hn_xg(oc, g0, n, gcol):
        S.add("act", CALL("mul", out=HN[:, oc, g0:g0 + n], in_=X[:, oc, g0:g0 + n], mul=prm_col(gcol + oc)),
              reads=T("x", oc, g0, g0 + n) + ["prm"], writes=T("hn", oc, g0, g0 + n))

    def post_mixer_epilogue(norm_idx, half, oc, gt, tn):
        hn_xg(oc, gt, tn, P_NG + norm_idx * 16)
        if half == 1:
            accum_sq(X[:, oc, gt:gt + tn], T("x", oc, gt, gt + tn), gt, tn)

    def setup_spatial():
        WSTF = vf32(o_FULLS, 1024)
        S.add("sp", CALL("dma_start", out=WSTF, in_=wst[:, :]), writes=["wstf"], slot="wstf")
        S.add("pool", CALL("affine_select", out=WSTF.rearrange("p (h t) -> p h t", t=128), in_=WSTF.rearrange("p (h t) -> p h t", t=128),
                                                pattern=[[0, 8], [1, 128]], compare_op=ALU.is_ge, fill=0.0, base=0,
                                                channel_multiplier=-1), reads=["wstf"], writes=["wstf"])
        S.add("dve", CALL("tensor_copy", out=WSTB.rearrange("p h t -> p (h t)"), in_=WSTF), reads=["wstf"], writes=["wstb"])
        for h in range(8):
            S.add("dve", CALL("tensor_scalar", out=WSSB[0:16, h, :], in0=IDF[0:16, 0:16], scalar1=BROW[0:16, B_W00 + h:B_W00 + h + 1],
                                                        scalar2=None, op0=ALU.mult), reads=["idf", "brow"], writes=[("wssb", h)])
            S.add("dve", CALL("tensor_scalar", out=BSBS[:, h, :], in0=ONES[:, 0:16], scalar1=BROW[:, B_BS0 + h:B_BS0 + h + 1],
                                                        scalar2=None, op0=ALU.mult), reads=["ones", "brow"], writes=[("bsbs", h)])

    def ffn(f, tiles, next_gcol=None, hook=None):
        for (h0, hn) in ((0, 16), (16, 16), (32, 12)):
            for j in range(hn // 2):
                n0 = (h0 + 2 * j) * 128
                s1 = load_w(w1v[f][:, :, n0:n0 + 256], 16)
                s3 = load_w(w3v[f][:, :, n0:n0 + 256], 16)
                for m in range(2):
                    gc = 2 * j + m
                    for (t0, tn) in tiles:
                        bA = nb()
                        mm_group(bA, tn, s1, m, 16, lambda k: HN[:, k, t0:t0 + tn], lambda k: T("hn", k, t0, t0 + tn),
                                 stat_tile=(t0, tn))
                        bB = nb()
                        mm_group(bB, tn, s3, m, 16, lambda k: HN[:, k, t0:t0 + tn], lambda k: T("hn", k, t0, t0 + tn))
                        q = sq_i[0] % 2
                        sq_i[0] += 1
                        rt = T("rstd", 0, t0, t0 + tn)
                        S.add("dve", CALL("tensor_tensor", out=SQ[q][:, 0:tn], in0=ps[bA][:, 0:tn], in1=RSTD[:, t0:t0 + tn], op=ALU.mult),
                              reads=[("ps", bA)] + rt, writes=[("sq", q)])
                        S.add("act", CALL("activation", out=SQ[q][:, 0:tn], in_=SQ[q][:, 0:tn], func=AF.Silu),
                              reads=[("sq", q)], writes=[("sq", q)])
                        S.add("dve", CALL("tensor_tensor", out=SQ[q][:, 0:tn], in0=SQ[q][:, 0:tn], in1=RSTD[:, t0:t0 + tn], op=ALU.mult),
                              reads=[("sq", q)] + rt, writes=[("sq", q)])
                        S.add("dve", CALL("tensor_tensor",
                            out=G[:, gc, t0:t0 + tn], in0=SQ[q][:, 0:tn], in1=ps[bB][:, 0:tn], op=ALU.mult),
                            reads=[("sq", q), ("ps", bB)], writes=T("g", gc, t0, t0 + tn))
            if h0 == 0 and hook is not None:
                hook()
            if h0 == 32:
                zero_rstd(tiles[0][0], W)
            for nb2 in range(8):
                s = load_w(w2v[f][:, h0:h0 + hn, nb2 * 256:(nb2 + 1) * 256], hn)
                for m in range(2):
                    oc = nb2 * 2 + m
                    for (t0, tn) in tiles:
                        b = nb()
                        mm_group(b, tn, s, m, hn, lambda k: G[:, k, t0:t0 + tn], lambda k: T("g", k, t0, t0 + tn))
                        S.add("dve", CALL("scalar_tensor_tensor",
                            out=X[:, oc, t0:t0 + tn], in0=ps[b][:, 0:tn], scalar=0.5, in1=X[:, oc, t0:t0 + tn],
                            op0=ALU.mult, op1=ALU.add),
                            reads=[("ps", b)] + T("x", oc, t0, t0 + tn), writes=T("x", oc, t0, t0 + tn))
                        if h0 == 32:
                            if next_gcol is not None:
                                hn_xg(oc, t0, tn, next_gcol)
                            accum_sq(X[:, oc, t0:t0 + tn], T("x", oc, t0, t0 + tn), t0, tn)
            if h0 == 32:
                flush_all()

    def mu_toks(t0, tn):
        return T(LNC[0]["name"], 0, t0, t0 + tn)

    def rs_toks(t0, tn):
        return T(LNC[0]["name"], 1, t0, t0 + tn)

    def ln_zero():
        S.add("dve", CALL("memset", LNC[0]["mu"], 0.0), writes=T(LNC[0]["name"], 0, 0, 536))
        S.add("dve", CALL("memset", LNC[0]["rs"], 0.0), writes=T(LNC[0]["name"], 1, 0, 536))

    def ln_flush():
        pass

    def ln_accum(src, src_toks, t0, tn):
        MU, RS = LNC[0]["mu"], LNC[0]["rs"]
        S.add("dve", CALL("tensor_tensor", out=MU[:, t0:t0 + tn], in0=MU[:, t0:t0 + tn], in1=src, op=ALU.add),
              reads=src_toks + mu_toks(t0, tn), writes=mu_toks(t0, tn))
        q = sq_i[0] % 2
        sq_i[0] += 1
        S.add("act", CALL("activation", out=SQ[q][:, 0:tn], in_=src, func=AF.Square), reads=src_toks, writes=[("sq", q)])
        S.add("dve", CALL("tensor_tensor", out=RS[:, t0:t0 + tn], in0=RS[:, t0:t0 + tn], in1=SQ[q][:, 0:tn], op=ALU.add),
              reads=[("sq", q)] + rs_toks(t0, tn), writes=rs_toks(t0, tn))

    def ln_stats(nch, tiles):
        inv = 1.0 / (nch * 128)
        MU, RS = LNC[0]["mu"], LNC[0]["rs"]
        for (t0, tn) in tiles:
            b1 = nb()
            S.add("pe", CALL("matmul", ps[b1][:, 0:tn], lhsT=ONES, rhs=MU[:, t0:t0 + tn], start=True, stop=True),
                  reads=["ones"] + mu_toks(t0, tn), writes=[("ps", b1)])
            b2 = nb()
            S.add("pe", CALL("matmul", ps[b2][:, 0:tn], lhsT=ONES, rhs=RS[:, t0:t0 + tn], start=True, stop=True),
                  reads=["ones"] + rs_toks(t0, tn), writes=[("ps", b2)])
            S.add("dve", CALL("tensor_scalar", out=MU[:, t0:t0 + tn], in0=ps[b1][:, 0:tn], scalar1=inv, scalar2=None, op0=ALU.mult),
                  reads=[("ps", b1)], writes=mu_toks(t0, tn))
            S.add("dve", CALL("tensor_tensor", out=SQ[0][:, 0:tn], in0=MU[:, t0:t0 + tn], in1=MU[:, t0:t0 + tn], op=ALU.mult),
                  reads=mu_toks(t0, tn), writes=[("sq", 0)])
            S.add("dve", CALL("scalar_tensor_tensor", out=RS[:, t0:t0 + tn], in0=ps[b2][:, 0:tn], scalar=inv, in1=SQ[0][:, 0:tn],
                              op0=ALU.mult, op1=ALU.subtract),
                  reads=[("ps", b2), ("sq", 0)], writes=rs_toks(t0, tn))
            rstd_from(RS[:, t0:t0 + tn], rs_toks(t0, tn), RS[:, t0:t0 + tn], rs_toks(t0, tn), 1.0)

    def ln_apply(buf, name, c, tiles, finish):
        MU, RS = LNC[0]["mu"], LNC[0]["rs"]
        for (t0, tn) in tiles:
            q = sq_i[0] % 2
            sq_i[0] += 1
            S.add("dve", CALL("tensor_tensor", out=SQ[q][:, 0:tn], in0=buf[:, c, t0:t0 + tn], in1=MU[:, t0:t0 + tn], op=ALU.subtract),
                  reads=T(name, c, t0, t0 + tn) + mu_toks(t0, tn), writes=[("sq", q)])
            S.add("dve", CALL("tensor_tensor", out=SQ[q][:, 0:tn], in0=SQ[q][:, 0:tn], in1=RS[:, t0:t0 + tn], op=ALU.mult),
                  reads=[("sq", q)] + rs_toks(t0, tn), writes=[("sq", q)])
            finish(c, t0, tn, SQ[q][:, 0:tn], ("sq", q))

    def layernorm_cols(buf, name, nch, tiles, finish):
        ln_stats(nch, tiles)
        for c in range(nch):
            ln_apply(buf, name, c, tiles, finish)

    def mixer0():
        LNC[0] = LNB
        cur_r = [o_R]

        def ralloc(n):
            o = cur_r[0]
            cur_r[0] += (n + 7) // 8 * 8
            assert cur_r[0] <= AW, "R overflow mixer0"
            return o

        AC = vf32(ralloc(8 * 536), 8 * 536).rearrange("p (c t) -> p c t", t=536)
        AO = vbf(ralloc(4 * 536), 4 * 536).rearrange("p (c t) -> p c t", t=536)
        BO = vbf(ralloc(4 * 536), 4 * 536).rearrange("p (c t) -> p c t", t=536)
        o_A = cur_r[0]
        AFB = vf32(ralloc(568), 568)
        AFB16s = [vbf(ralloc(284), 284) for _ in range(2)]
        DG = vbf(ralloc(31 * 64), 31 * 64).rearrange("p (k m) -> p k m", m=128)
        FULLS16s = [vbf(ralloc(248), 248).rearrange("p (k b) -> p k b", b=16) for _ in range(2)]
        cur_r[0] = o_A
        CHF = vf32(ralloc(568), 568)
        BG = vf32(ralloc(536), 536)
        ACCB = vf32(ralloc(536), 536)

        for half in range(2):
            g0 = 536 * half
            tl_in = [(0, 268), (268, 268)]
            if half == 0:
                tl_out = [(32, 252), (284, 252)]
                p0, pn = 32, 504
            else:
                tl_out = [(0, 268), (268, 268)]
                p0, pn = 0, 520
            ln_zero()
            if half == 0:
                S.add("dve", CALL("memset", AFB[:, 0:32], 0.0), writes=[("afb", 0)])
            slots_a = {}

            def A1(c):
                cb, m = c // 2, c % 2
                if m == 0:
                    slots_a[cb] = (load_w(abinv[:, :, cb * 256:(cb + 1) * 256], 16),
                                   load_w(abinv[:, :, 1024 + cb * 256:1024 + (cb + 1) * 256], 16))
                s_pa, s_ga = slots_a[cb]
                AFB16 = AFB16s[c % 2]
                FULLS16 = FULLS16s[c % 2]
                if half == 1:
                    S.add("act", CALL("copy", out=AFB[:, 0:32], in_=ACARRY[:, c, :]),
                          reads=[("acarry", c)], writes=[("afb", 0)])
                for (t0, tn) in tl_in:
                    gt = g0 + t0
                    b1 = nb()
                    mm_group(b1, tn, s_pa, m, 16, lambda k: HN[:, k, gt:gt + tn], lambda k: T("hn", k, gt, gt + tn))
                    b2 = nb()
                    mm_group(b2, tn, s_ga, m, 16, lambda k: HN[:, k, gt:gt + tn], lambda k: T("hn", k, gt, gt + tn))
                    tq = 0 if t0 == 0 else 1
                    rt = T("rstd", 0, gt, gt + tn)
                    S.add("dve", CALL("tensor_tensor", out=SQ[tq][:, 0:tn], in0=ps[b2][:, 0:tn], in1=RSTD[:, gt:gt + tn], op=ALU.mult),
                          reads=[("ps", b2)] + rt, writes=[("sq", tq)])
                    S.add("act", CALL("activation", out=SQ[tq][:, 0:tn], in_=SQ[tq][:, 0:tn], func=AF.Sigmoid),
                          reads=[("sq", tq)], writes=[("sq", tq)])
                    S.add("dve", CALL("tensor_tensor", out=SQ[tq][:, 0:tn], in0=SQ[tq][:, 0:tn], in1=RSTD[:, gt:gt + tn], op=ALU.mult),
                          reads=[("sq", tq)] + rt, writes=[("sq", tq)])
                    S.add("dve", CALL("tensor_tensor", out=AFB[:, 32 + t0:32 + t0 + tn], in0=SQ[tq][:, 0:tn],
                                      in1=ps[b1][:, 0:tn], op=ALU.mult),
                          reads=[("ps", b1), ("sq", tq)], writes=[("afb", 1 + t0)])
                afb_all = [("afb", 0), ("afb", 1), ("afb", 269)]
                if half == 0:
                    S.add("act", CALL("copy", out=ACARRY[:, c, :], in_=AFB[:, 536:568]),
                          reads=afb_all, writes=[("acarry", c)])
                S.add("act", CALL("copy", out=AFB16, in_=AFB), reads=afb_all, writes=[("afb16", c % 2)])
                if half == 1:
                    S.add("sp", CALL("dma_start", out=caP[c * 128:(c + 1) * 128, :], in_=AFB[:, 32 + 490:32 + 520]),
                          reads=afb_all, writes=[("caP", c)], slot=("caP", c % 2))
                    S.add("sp", CALL("dma_start", out=FULLS[:, 0:30, :], in_=sa[c * 128:(c + 1) * 128, :, :]),
                          writes=[("fulls", 0)], slot=("fulls", 0))
                    S.add("act", CALL("copy", out=FULLS[:, 30, :], in_=AFB[:, 32 + 520:32 + 536]),
                          reads=afb_all, writes=[("fulls", 1)])
                    S.add("act", CALL("copy", out=FULLS16, in_=FULLS), reads=[("fulls", 0), ("fulls", 1)], writes=[("fulls16", c % 2)])
                    S.add("sp", CALL("dma_start", out=caS[c * 128:(c + 1) * 128, :, :], in_=FULLS[:, 1:31, :]),
                          reads=[("fulls", 0), ("fulls", 1)], writes=[("caS", c)], slot=("caS", c % 2))

            def DGB(c):
                for k in range(31):
                    if k % 2 == 0:
                        S.add("dve", CALL("tensor_scalar", out=DG[:, k, :], in0=IDF, scalar1=prm_col(P_ACW + c * 31 + k), scalar2=None,
                                          op0=ALU.mult), reads=["idf", "prm"], writes=[("dg", k)])
                    else:
                        S.add("act", CALL("mul", out=DG[:, k, :], in_=IDF, mul=prm_col(P_ACW + c * 31 + k)),
                              reads=["idf", "prm"], writes=[("dg", k)])

            def A2(c):
                AFB16 = AFB16s[c % 2]
                FULLS16 = FULLS16s[c % 2]
                for ti, (t0, tn) in enumerate(tl_out):
                    bank = nb()
                    pn_t = tn
                    if half == 1 and ti == 1:
                        pn_t = tn - 16
                    for k in range(31):
                        S.add("pe", CALL("matmul", ps[bank][:, 0:pn_t], lhsT=DG[:, k, :], rhs=AFB16[:, 2 + t0 + k:2 + t0 + k + pn_t],
                                         start=(k == 0), stop=(k == 30), skip_group_check=True),
                              reads=[("dg", k), ("afb16", c % 2)], writes=[("ps", bank)])
                    if pn_t != tn:
                        for k in range(31):
                            S.add("pe", CALL("matmul", ps[bank][:, pn_t:tn], lhsT=DG[:, k, :], rhs=FULLS16[:, k, :],
                                             start=(k == 0), stop=(k == 30), skip_group_check=True),
                                  reads=[("dg", k), ("fulls16", c % 2)], writes=[("ps", bank)])
                    S.add("act", CALL("activation", out=AC[:, c, t0:t0 + tn], in_=ps[bank][:, 0:tn], func=AF.Identity,
                                      bias=prm_col(P_ACB + c), scale=1.0),
                          reads=[("ps", bank), "prm"], writes=T("ac", c, t0, t0 + tn))
                    ln_accum(AC[:, c, t0:t0 + tn], T("ac", c, t0, t0 + tn), t0, tn)

            A1(0)
            DGB(0)
            for c in range(8):
                if c + 1 < 8:
                    A1(c + 1)
                A2(c)
                if c + 1 < 8:
                    DGB(c + 1)
            def fin_a(c, t0, tn, tmp, tmptok):
                S.add("act", CALL("activation",
                    out=AO[:, c, t0:t0 + tn], in_=tmp, func=AF.Silu, bias=prm_col(P_ALB + c), scale=prm_col(P_ALG + c)),
                    reads=[tmptok, "prm"], writes=T("ao", c, t0, t0 + tn))
            S.barrier(dma_slots=[("caP", 0), ("caP", 1), ("cbP", 0), ("cbP", 1)])
            ln_stats(8, tl_out)
            for cb in range(4):
                s_bg = load_w(abinv[:, :, 2048 + cb * 256:2048 + (cb + 1) * 256], 16)
                s_cg = load_w(abinv[:, :, 3072 + cb * 256:3072 + (cb + 1) * 256], 16)
                s_hb = load_w(abinv[:, :, 4096 + cb * 256:4096 + (cb + 1) * 256], 16)
                for m in range(2):
                    c = cb * 2 + m
                    if half == 1:
                        S.add("act", CALL("copy", out=CHF[:, 0:32], in_=CHCARRY[:, c, :]),
                              reads=[("chcarry", c)], writes=[("chf", 0)])
                    for (t0, tn) in tl_in:
                        gt = g0 + t0
                        b1 = nb()
                        mm_group(b1, tn, s_bg, m, 16, lambda k: HN[:, k, gt:gt + tn], lambda k: T("hn", k, gt, gt + tn))
                        b2 = nb()
                        mm_group(b2, tn, s_cg, m, 16, lambda k: HN[:, k, gt:gt + tn], lambda k: T("hn", k, gt, gt + tn))
                        b3 = nb()
                        mm_group(b3, tn, s_hb, m, 16, lambda k: HN[:, k, gt:gt + tn], lambda k: T("hn", k, gt, gt + tn))
                        rt = T("rstd", 0, gt, gt + tn)
                        S.add("dve", CALL("tensor_tensor", out=BG[:, t0:t0 + tn], in0=ps[b1][:, 0:tn], in1=RSTD[:, gt:gt + tn], op=ALU.mult),
                              reads=[("ps", b1)] + rt, writes=[("bg", t0)])
                        tq = 0 if t0 == 0 else 1
                        S.add("dve", CALL("tensor_tensor", out=SQ[tq][:, 0:tn], in0=ps[b3][:, 0:tn], in1=RSTD[:, gt:gt + tn], op=ALU.mult),
                              reads=[("ps", b3)] + rt, writes=[("sq", tq)])
                        S.add("dve", CALL("tensor_tensor", out=CHF[:, 32 + t0:32 + t0 + tn], in0=ps[b2][:, 0:tn], in1=RSTD[:, gt:gt + tn],
                                          op=ALU.mult),
                              reads=[("ps", b2)] + rt, writes=[("chf", 1 + t0)])
                        S.add("dve", CALL("tensor_tensor", out=CHF[:, 32 + t0:32 + t0 + tn], in0=CHF[:, 32 + t0:32 + t0 + tn],
                                          in1=SQ[tq][:, 0:tn], op=ALU.mult),
                              reads=[("sq", tq), ("chf", 1 + t0)], writes=[("chf", 1 + t0)])
                    chf_all = [("chf", 0), ("chf", 1), ("chf", 269)]
                    if half == 0:
                        S.add("act", CALL("copy", out=CHCARRY[:, c, :], in_=CHF[:, 536:568]),
                              reads=chf_all, writes=[("chcarry", c)])
                    acc = ACCB[:, p0:p0 + pn]
                    for k in range(3):
                        src = CHF[:, 30 + p0 + k:30 + p0 + k + pn]
                        wk = prm_col(P_BCW + c * 3 + k)
                        if k == 0:
                            S.add("dve", CALL("tensor_scalar", out=acc, in0=src, scalar1=wk, scalar2=None, op0=ALU.mult),
                                  reads=chf_all + ["prm"], writes=[("accb", 0)])
                        else:
                            S.add("dve", CALL("scalar_tensor_tensor",
                                out=acc, in0=src, scalar=wk, in1=acc, op0=ALU.mult, op1=ALU.add),
                                reads=chf_all + [("accb", 0), "prm"], writes=[("accb", 0)])
                    bread = [("accb", 0)]
                    if half == 1:
                        S.add("sp", CALL("dma_start", out=cbP[c * 128:(c + 1) * 128, :], in_=CHF[:, 32 + 518:32 + 520]),
                              reads=chf_all, writes=[("cbP", c)], slot=("cbP", c % 2))
                        S.add("sp", CALL("dma_start", out=FULLSB[:, 0:2, :], in_=sb[c * 128:(c + 1) * 128, :, :]),
                              writes=[("fullsb", 0)], slot=("fullsb", 0))
                        S.add("act", CALL("copy", out=FULLSB[:, 2, :], in_=CHF[:, 32 + 520:32 + 536]),
                              reads=chf_all, writes=[("fullsb", 1)])
                        accs = ACCB[:, 520:536]
                        for k in range(3):
                            wk = prm_col(P_BCW + c * 3 + k)
                            src = FULLSB[:, k, :]
                            if k == 0:
                                S.add("dve", CALL("tensor_scalar", out=accs, in0=src, scalar1=wk, scalar2=None, op0=ALU.mult),
                                      reads=[("fullsb", 0), ("fullsb", 1), "prm"], writes=[("accb", 1)])
                            else:
                                S.add("dve", CALL("scalar_tensor_tensor",
                                    out=accs, in0=src, scalar=wk, in1=accs, op0=ALU.mult, op1=ALU.add),
                                    reads=[("fullsb", 0), ("fullsb", 1), ("accb", 1), "prm"], writes=[("accb", 1)])
                        S.add("sp", CALL("dma_start", out=cbS[c * 128:(c + 1) * 128, :, :], in_=FULLSB[:, 1:3, :]),
                              reads=[("fullsb", 0), ("fullsb", 1)], writes=[("cbS", c)], slot=("cbS", c % 2))
                        bread = [("accb", 0), ("accb", 1)]
                    o0 = tl_out[0][0]
                    on = tl_out[-1][0] + tl_out[-1][1] - o0
                    S.add("dve", CALL("tensor_tensor", out=BO[:, c, o0:o0 + on], in0=ACCB[:, o0:o0 + on],
                                                                              in1=BG[:, o0:o0 + on], op=ALU.mult),
                          reads=bread + [("bg", 0), ("bg", 268)], writes=T("bo", c, o0, o0 + on))
                    ln_apply(AC, "ac", c, tl_out, fin_a)
            if half == 1:
                half0_sq(32, 536)
                zero_rstd(536, W)
            for nb2 in range(8):
                s = load_w(aboutv[:, :, nb2 * 256:(nb2 + 1) * 256], 16)
                for m in range(2):
                    oc = nb2 * 2 + m
                    for (t0, tn) in tl_out:
                        b = nb()
                        mm_group(b, tn, s, m, 16,
                                 lambda k: (AO[:, k, t0:t0 + tn] if k < 8 else BO[:, k - 8, t0:t0 + tn]),
                                 lambda k: (T("ao", k, t0, t0 + tn) if k < 8 else T("bo", k - 8, t0, t0 + tn)))
                        gt = g0 + t0
                        S.add("dve", CALL("tensor_tensor",
                            out=X[:, oc, gt:gt + tn], in0=ps[b][:, 0:tn], in1=X[:, oc, gt:gt + tn], op=ALU.add),
                            reads=[("ps", b)] + T("x", oc, gt, gt + tn), writes=T("x", oc, gt, gt + tn))
                        post_mixer_epilogue(2, half, oc, gt, tn)
            if half == 1:
                flush_all()
            S.barrier(dma_slots=[("caP", 0), ("caP", 1), ("cbP", 0), ("cbP", 1)])
        S.add("sp", CALL("dma_start", out=BROW, in_=brow.partition_broadcast(128)),
              writes=["brow"] + T("lnb0", 0, 0, 536) + T("lnb0", 1, 0, 536), slot="brow")

    def mixer1():
        LNC[0] = LNB1
        S.barrier(dma_slots=[("caS", 0), ("caS", 1), ("cbS", 0), ("cbS", 1), ("fulls", 0), ("fullsb", 0)])
        o_V = o_R
        V = vf32(o_V, 16 * 528).rearrange("p (c t) -> p c t", t=528)
        o_VNB = o_R + 16 * 528
        VNB = vbf(o_VNB, 8 * 528).rearrange("p (c t) -> p c t", t=528)
        assert o_VNB + 8 * 528 <= AW, "R overflow mixer1"
        VT = vbf(o_V, 5 * 1024).rearrange("p (n d) -> p n d", d=2048)
        UT = vf32(o_V + 5120, 528)
        T2 = vf32(o_V + 5120 + 528, 528)
        PT = [ps[6][:, :].bitcast(BF16), ps[7][:, :].bitcast(BF16)]
        for half in range(2):
            if half == 0:
                g0 = 32
                hw = 512
                tiles = [(0, 512)]
                chunks = [(i * 128, 128) for i in range(4)]
                sgroups = [(0, 512, [0, 1, 2, 3])]
            else:
                g0 = 544
                hw = 528
                tiles = [(0, 264), (264, 264)]
                chunks = [(i * 128, 128) for i in range(4)] + [(512, 16)]
                sgroups = [(0, 512, [0, 1, 2, 3]), (512, 16, [4])]
            ln_zero()
            for nb2 in range(8):
                s = load_w(cinv[:, :, 2048 + nb2 * 256:2048 + (nb2 + 1) * 256], 16)
                for m in range(2):
                    oc = nb2 * 2 + m
                    for (t0, tn) in tiles:
                        gt = g0 + t0
                        b = nb()
                        mm_group(b, tn, s, m, 16, lambda k: HN[:, k, gt:gt + tn], lambda k: T("hn", k, gt, gt + tn))
                        q = sq_i[0] % 2
                        sq_i[0] += 1
                        S.add("dve", CALL("tensor_tensor", out=SQ[q][:, 0:tn], in0=ps[b][:, 0:tn], in1=RSTD[:, gt:gt + tn], op=ALU.mult),
                              reads=[("ps", b)] + T("rstd", 0, gt, gt + tn), writes=[("sq", q)])
                        S.add("act", CALL("activation",
                            out=V[:, oc, t0:t0 + tn], in_=SQ[q][:, 0:tn], func=AF.Gelu, bias=prm_col(P_CBI + 16 + oc), scale=1.0),
                            reads=[("sq", q), "prm"], writes=T("v", oc, t0, t0 + tn))
                        ln_accum(V[:, oc, t0:t0 + tn], T("v", oc, t0, t0 + tn), t0, tn)
            def fin_v(c, t0, tn, tmp, tmptok):
                S.add("act", CALL("activation",
                    out=VNB[:, c, t0:t0 + tn], in_=tmp, func=AF.Identity, bias=prm_col(P_CLB + c), scale=prm_col(P_CLG + c)),
                    reads=[tmptok, "prm"], writes=T("vnb", c, t0, t0 + tn))
                if t0 + tn == 528:
                    S.add("act", CALL("activation",
                        out=VS[:, c, :], in_=tmp[:, tn - 16:tn], func=AF.Identity, bias=prm_col(P_CLB + c), scale=prm_col(P_CLG + c)),
                        reads=[tmptok, "prm"], writes=[("vs", c)])
            layernorm_cols(V, "v", 16, tiles, fin_v)
            if half == 1:
                S.add("sp", CALL("dma_start", out=vSv, in_=VS), reads=[("vs", c) for c in range(16)], writes=["vS"], slot="vS")
            S.barrier()
            ti = 0
            for n, (c0, cw) in enumerate(chunks):
                for q4 in range(4):
                    pb = ti % 2
                    ti += 1
                    for qq in range(4):
                        oc = q4 * 4 + qq
                        S.add("pe", CALL("transpose",
                            out=PT[pb][0:cw, qq * 128:(qq + 1) * 128], in_=VNB[:, oc, c0:c0 + cw], identity=IDB),
                            reads=T("vnb", oc, c0, c0 + cw) + ["idb"], writes=[("ps", 6 + pb)])
                    if pb == 0:
                        S.add("act", CALL("copy", out=VT[0:cw, n, q4 * 512:(q4 + 1) * 512], in_=PT[pb][0:cw, 0:512]),
                              reads=[("ps", 6 + pb)], writes=[("vt", n, q4)])
                    else:
                        S.add("dve", CALL("tensor_copy", out=VT[0:cw, n, q4 * 512:(q4 + 1) * 512], in_=PT[pb][0:cw, 0:512]),
                              reads=[("ps", 6 + pb)], writes=[("vt", n, q4)])
            US = VNB
            for nb2 in range(8):
                s = load_w(cinv[:, :, nb2 * 256:(nb2 + 1) * 256], 16)
                for m in range(2):
                    oc = nb2 * 2 + m
                    h = oc // 2
                    for (t0, tn) in tiles:
                        gt = g0 + t0
                        b = nb()
                        mm_group(b, tn, s, m, 16, lambda k: HN[:, k, gt:gt + tn], lambda k: T("hn", k, gt, gt + tn))
                        q = sq_i[0] % 2
                        sq_i[0] += 1
                        S.add("dve", CALL("tensor_tensor", out=SQ[q][:, 0:tn], in0=ps[b][:, 0:tn], in1=RSTD[:, gt:gt + tn], op=ALU.mult),
                              reads=[("ps", b)] + T("rstd", 0, gt, gt + tn), writes=[("sq", q)])
                        S.add("act", CALL("activation",
                            out=UT[:, t0:t0 + tn], in_=SQ[q][:, 0:tn], func=AF.Gelu, bias=prm_col(P_CBI + oc), scale=1.0),
                            reads=[("sq", q), "prm"], writes=[("ut", t0)])
                    t2toks = []
                    for (s0, sn, cl) in sgroups:
                        b2 = nb()
                        first = True
                        for n in cl:
                            c0, cw = chunks[n]
                            lo = c0 - s0
                            rhs = WSTB[0:cw, h, 0:cw] if cw == 128 else WSSB[0:cw, h, 0:cw]
                            rtok = ["wstb"] if cw == 128 else [("wssb", h)]
                            S.add("pe", CALL("matmul",
                                ps[b2][:, lo:lo + cw], lhsT=VT[0:cw, n, oc * 128:(oc + 1) * 128], rhs=rhs, start=first, stop=True,
                                skip_group_check=True),
                                reads=[("vt", n, oc // 4)] + rtok, writes=[("ps", b2)])
                            first = False
                        if sn == 16:
                            S.add("dve", CALL("tensor_tensor",
                                out=T2[:, s0:s0 + sn], in0=ps[b2][:, 0:sn], in1=BSBS[:, h, :], op=ALU.add),
                                reads=[("ps", b2), ("bsbs", h)], writes=[("t2", s0)])
                        else:
                            bsrc = BROW[:, B_BS + h * 128:B_BS + (h + 1) * 128]
                            bb = bass.AP(bsrc.tensor, bsrc.offset, [list(bsrc.ap[0]), [0, 4], list(bsrc.ap[1])])
                            S.add("dve", CALL("tensor_tensor",
                                out=T2[:, s0:s0 + sn].rearrange("p (n t) -> p n t", t=128),
                                in0=ps[b2][:, 0:sn].rearrange("p (n t) -> p n t", t=128), in1=bb, op=ALU.add),
                                reads=[("ps", b2), "brow"], writes=[("t2", s0)])
                        t2toks.append(("t2", s0))
                    S.add("dve", CALL("tensor_tensor",
                        out=US[:, oc, 0:hw], in0=T2[:, 0:hw], in1=UT[:, 0:hw], op=ALU.mult),
                        reads=t2toks + [("ut", t0) for (t0, _) in tiles], writes=T("vnb", oc, 0, hw))
            if half == 1:
                half0_sq(32, 544)
                zero_rstd(544, W)
            for nb2 in range(8):
                s = load_w(coutv[:, :, nb2 * 256:(nb2 + 1) * 256], 16)
                for m in range(2):
                    oc = nb2 * 2 + m
                    for (t0, tn) in tiles:
                        b = nb()
                        mm_group(b, tn, s, m, 16, lambda k: US[:, k, t0:t0 + tn], lambda k: T("vnb", k, t0, t0 + tn))
                        gt = g0 + t0
                        S.add("dve", CALL("tensor_tensor",
                            out=X[:, oc, gt:gt + tn], in0=ps[b][:, 0:tn], in1=X[:, oc, gt:gt + tn], op=ALU.add),
                            reads=[("ps", b)] + T("x", oc, gt, gt + tn), writes=T("x", oc, gt, gt + tn))
                        post_mixer_epilogue(5, half, oc, gt, tn)
            if half == 1:
                flush_all()
            S.barrier()

    TL_ALL = [(0, 358), (358, 358), (716, 356)]
    TL_MAIN = [(32, 348), (380, 346), (726, 346)]

    def dump_x():
        for k in range(16):
            S.add("sp", CALL("dma_start", out=yTv[:, k, :], in_=X[:, k, 32:W]), reads=T("x", k, 32, W),
                  writes=[("y", k)], slot=("y", k % 4))

    def half0_sq(c_lo, c_hi):
        zero_rstd(c_lo, c_hi)
        for k in range(16):
            accum_sq(X[:, k, c_lo:c_hi], T("x", k, c_lo, c_hi), c_lo, c_hi - c_lo)
        flush_all()

    def run():
        zero_rstd(0, W)
        for (t0, tn) in TL_ALL:
            for k in range(16):
                if k % 2 == 0:
                    hn_xg(k, t0, tn, P_NG + 0 * 16)
                else:
                    S.add("dve", CALL("tensor_scalar", out=HN[:, k, t0:t0 + tn], in0=X[:, k, t0:t0 + tn], scalar1=prm_col(P_NG + k),
                                      scalar2=None, op0=ALU.mult),
                          reads=T("x", k, t0, t0 + tn) + ["prm"], writes=T("hn", k, t0, t0 + tn))
            for k in range(16):
                accum_sq(X[:, k, t0:t0 + tn], T("x", k, t0, t0 + tn), t0, tn)
        flush_all()
        rstd_defer(TL_ALL, per_tile=True)
        ffn(0, TL_ALL, next_gcol=P_NG + 1 * 16, hook=setup_spatial)
        S.barrier()
        if stop == "f0":
            return dump_x()
        rstd_defer(TL_ALL)
        mixer0()
        if stop == "m0":
            return dump_x()
        rstd_defer(TL_MAIN)
        ffn(1, TL_MAIN, next_gcol=P_NG + 3 * 16)
        if stop == "f1":
            return dump_x()
        rstd_defer(TL_MAIN)
        ffn(2, TL_MAIN, next_gcol=P_NG + 4 * 16)
        rstd_defer(TL_MAIN)
        mixer1()
        if stop == "m1":
            return dump_x()
        rstd_defer(TL_MAIN)
        ffn(3, TL_MAIN)
        S.barrier()
        YS = vf32(o_R, 12 * 512).rearrange("p (q t) -> p q t", t=512)
        cnt = [0]

        for (t0, tn) in TL_MAIN:
            bank = nb()
            S.add("pe", CALL("matmul", ps[bank][:, 0:tn], lhsT=ONES, rhs=RSTD[:, t0:t0 + tn], start=True, stop=True),
                  reads=["ones"] + T("rstd", 0, t0, t0 + tn), writes=[("ps", bank)])
            rstd_from(ps[bank][:, 0:tn], [("ps", bank)], RSTD[:, t0:t0 + tn], T("rstd", 0, t0, t0 + tn), 1.0 / D)
            for k in range(16):
                q = cnt[0] % 12
                cnt[0] += 1
                if k % 3 == 2:
                    S.add("act", CALL("mul", out=YS[:, q, 0:tn], in_=X[:, k, t0:t0 + tn], mul=prm_col(P_FG + k)),
                          reads=T("x", k, t0, t0 + tn) + ["prm"], writes=[("ys", q)])
                    S.add("pool", CALL("tensor_tensor", out=YS[:, q, 0:tn], in0=YS[:, q, 0:tn], in1=RSTD[:, t0:t0 + tn], op=ALU.mult),
                          reads=[("ys", q)] + T("rstd", 0, t0, t0 + tn), writes=[("ys", q)])
                else:
                    S.add("dve", CALL("scalar_tensor_tensor",
                        out=YS[:, q, 0:tn], in0=X[:, k, t0:t0 + tn], scalar=prm_col(P_FG + k), in1=RSTD[:, t0:t0 + tn],
                        op0=ALU.mult, op1=ALU.mult),
                        reads=T("x", k, t0, t0 + tn) + T("rstd", 0, t0, t0 + tn) + ["prm"], writes=[("ys", q)])
                S.add(("sp", "act", "pool")[k % 3], CALL("dma_start", out=yTv[:, k, t0 - 32:t0 - 32 + tn], in_=YS[:, q, 0:tn]),
                      reads=[("ys", q)], writes=[("y", k, t0)], slot=("y", q))

    run()
    finals = [k for k in S.slot_cnt if k[0] in ("y", "caP", "cbP", "caS", "cbS") or k == "vS"]
    stats = S.emit(nc, final_slots=finals)
    return nc, stats, S.n


def _pcol(v):
    v = np.asarray(v, np.float32)
    return np.ascontiguousarray(v.reshape(-1, 128).T)


def make_in_maps(inp):
    f32 = np.float32
    xp = np.asarray(inp["x_prompt"], f32)
    xs = np.asarray(inp["x_sample"], f32)
    sa = np.asarray(inp["state_conv_a"], f32)[0]
    sb = np.asarray(inp["state_conv_b"], f32)[0]
    shared = {
        "w1": np.ascontiguousarray(np.asarray(inp["ffn_w1"], f32).reshape(4, D, DFF)),
        "w3": np.ascontiguousarray(np.asarray(inp["ffn_w3"], f32).reshape(4, D, DFF)),
        "w2": np.ascontiguousarray(np.asarray(inp["ffn_w2"], f32).reshape(4, DFF, D)),
        "abin": np.ascontiguousarray(np.asarray(inp["ab_w_in"], f32)[0]),
        "about": np.ascontiguousarray(np.asarray(inp["ab_w_out"], f32)[0]),
        "cin": np.ascontiguousarray(np.asarray(inp["c_w_in"], f32)[0]),
        "cout": np.ascontiguousarray(np.asarray(inp["c_w_out"], f32)[0]),
    }
    prm = np.zeros((128, P_N), f32)
    ng = np.asarray(inp["norm_g"], f32).reshape(6, D)
    for i in range(6):
        prm[:, P_NG + i * 16:P_NG + (i + 1) * 16] = _pcol(ng[i])
    prm[:, P_FG:P_FG + 16] = _pcol(inp["final_g"])
    acw = np.asarray(inp["a_conv_w"], f32)[0]
    prm[:, P_ACW:P_ACW + 248] = acw.T.reshape(8, 128, 31).transpose(1, 0, 2).reshape(128, 248)
    prm[:, P_ACB:P_ACB + 8] = _pcol(np.asarray(inp["a_conv_b"])[0])
    prm[:, P_ALG:P_ALG + 8] = _pcol(np.asarray(inp["a_ln_g"])[0])
    prm[:, P_ALB:P_ALB + 8] = _pcol(np.asarray(inp["a_ln_b"])[0])
    bcw = np.asarray(inp["b_conv_w"], f32)[0]
    prm[:, P_BCW:P_BCW + 24] = bcw.T.reshape(8, 128, 3).transpose(1, 0, 2).reshape(128, 24)
    prm[:, P_CBI:P_CBI + 32] = _pcol(np.asarray(inp["c_b_in"])[0])
    prm[:, P_CLG:P_CLG + 16] = _pcol(np.asarray(inp["c_ln_g"])[0])
    prm[:, P_CLB:P_CLB + 16] = _pcol(np.asarray(inp["c_ln_b"])[0])
    cws = np.asarray(inp["c_w_s"], f32)[0]
    wst = np.ascontiguousarray(cws.transpose(2, 0, 1).reshape(128, 1024))
    cbs = np.asarray(inp["c_b_s"], f32)[0]
    brow = np.zeros((1, B_N), f32)
    brow[0, B_W00:B_W00 + 8] = cws[:, 0, 0]
    brow[0, B_BS0:B_BS0 + 8] = cbs[:, 0]
    brow[0, B_BS:B_BS + 1024] = cbs.reshape(-1)
    shared.update(prm=prm, wst=wst, brow=brow)
    maps = []
    for i in range(NCORES):
        b, half = i // 2, i % 2
        xt = np.zeros((D, W), f32)
        if half == 1:
            xt[:, 0:HALO] = xp[b, NPR - HALO:NPR, :].T
        xt[:, HALO:HALO + NPR] = xp[b, half * NPR:(half + 1) * NPR, :].T
        xt[:, HALO + NPR:W] = xs[i * NSM:(i + 1) * NSM, 0, :].T
        m = dict(shared)
        m["xT"] = xt
        m["sa"] = np.ascontiguousarray(sa[i * NSM:(i + 1) * NSM].transpose(2, 1, 0))
        m["sb"] = np.ascontiguousarray(sb[i * NSM:(i + 1) * NSM].transpose(2, 1, 0))
        maps.append(m)
    return maps


_CACHE = {}


def kernel(**inp):
    if "nc" not in _CACHE:
        _CACHE["nc"] = build_program(DEBUG_STOP)[0]
    nc = _CACHE["nc"]
    maps = make_in_maps(inp)
    res = run_bass_kernel_spmd(nc, maps, core_ids=list(range(NCORES)))
    R = res.results
    f32 = np.float32
    y_prompt = np.zeros((4, 2048, D), f32)
    y_sample = np.zeros((128, 1, D), f32)
    ca_p = np.zeros((1, 4, 30, 1024), f32)
    cb_p = np.zeros((1, 4, 2, 1024), f32)
    ca_s = np.zeros((1, 128, 30, 1024), f32)
    cb_s = np.zeros((1, 128, 2, 1024), f32)
    v_s = np.zeros((1, 128, 1, D), f32)
    for i in range(NCORES):
        b, half = i // 2, i % 2
        yt = np.asarray(R[i]["yT"], f32)
        y_prompt[b, half * NPR:(half + 1) * NPR, :] = yt[:, 0:NPR].T
        y_sample[i * NSM:(i + 1) * NSM, 0, :] = yt[:, NPR:].T
        if half == 1:
            ca_p[0, b] = np.asarray(R[i]["caP"], f32).T
            cb_p[0, b] = np.asarray(R[i]["cbP"], f32).T
        ca_s[0, i * NSM:(i + 1) * NSM] = np.asarray(R[i]["caS"], f32).transpose(2, 1, 0)
        cb_s[0, i * NSM:(i + 1) * NSM] = np.asarray(R[i]["cbS"], f32).transpose(2, 1, 0)
        v_s[0, i * NSM:(i + 1) * NSM, 0, :] = np.asarray(R[i]["vS"], f32).T
    return (y_prompt, y_sample, ca_p, cb_p, ca_s, cb_s, v_s)
```

```python
from contextlib import ExitStack
import numpy as np
import concourse.bass as bass
import concourse.mybir as mybir
from concourse.bass_utils import run_bass_kernel_spmd

F32 = mybir.dt.float32
F32R = mybir.dt.float32r
BF16 = mybir.dt.bfloat16
ALU = mybir.AluOpType
AF = mybir.ActivationFunctionType

ENGS = ("pe", "act", "dve", "pool", "sp")
NCORES = 8
D = 2048
DFF = 5632
W = 1072
HALO = 32
NPR = 1024
NSM = 16
TOUT = NPR + NSM
EPS = 1e-6
DEBUG_STOP = None


def CALL(name, *a, **kw):
    return (name, a, kw)


class Op:
    __slots__ = ("eng", "fn", "deps", "needed", "sig", "dma", "slot", "val")


class Sched:
    def __init__(self):
        self.ops = {e: [] for e in ENGS}
        self.lastw = {}
        self.readers = {}
        self.slot_cnt = {}
        self.last_dma = {}
        self.n = 0

    def add(self, eng, fn, reads=(), writes=(), slot=None, extra_deps=()):
        o = Op()
        o.eng = eng
        o.fn = fn
        o.needed = False
        o.sig = 0
        o.dma = slot is not None
        o.slot = slot
        o.val = 0
        if o.dma:
            c = self.slot_cnt.get(slot, 0) + 1
            self.slot_cnt[slot] = c
            o.val = 16 * c
            self.last_dma[slot] = o
        deps = {}
        lastw = self.lastw
        readers = self.readers
        for t in reads:
            w = lastw.get(t)
            if w is not None:
                deps[id(w)] = w
        for t in writes:
            w = lastw.get(t)
            if w is not None:
                deps[id(w)] = w
            r = readers.get(t)
            if r:
                for x in r.values():
                    deps[id(x)] = x
        for d in extra_deps:
            deps[id(d)] = d
        dl = []
        for d in deps.values():
            if d is o:
                continue
            if (not d.dma) and (not o.dma) and d.eng == "pe" and eng == "pe":
                continue
            d.needed = True
            dl.append(d)
        o.deps = dl
        rkey = ("q", self.n) if o.dma else eng
        for t in reads:
            r = readers.get(t)
            if r is None:
                readers[t] = {rkey: o}
            else:
                r[rkey] = o
        for t in writes:
            lastw[t] = o
            readers[t] = None
        self.ops[eng].append(o)
        self.n += 1
        return o

    def barrier(self, engs=("pe", "act", "dve"), dma_slots=()):
        last = {}
        for e in engs:
            last[e] = None
            for o in reversed(self.ops[e]):
                if o.fn is not None:
                    last[e] = o
                    break
        dd = [self.last_dma[k] for k in dma_slots if k in self.last_dma]
        for e in engs:
            ds = [last[x] for x in engs if x != e and last[x] is not None]
            self.add(e, None, extra_deps=ds + dd)

    def emit(self, nc, final_slots=()):
        for e in ENGS:
            k = 0
            for o in self.ops[e]:
                if o.needed and not o.dma:
                    k += 1
                    o.sig = k
        stats = {"waits": 0, "ins": 0}
        with ExitStack() as st:
            esem = {e: st.enter_context(nc.semaphore("s_" + e)) for e in ENGS}
            ssem = {}
            for i, k in enumerate(self.slot_cnt):
                ssem[k] = st.enter_context(nc.semaphore("d%d" % i))
            block = st.enter_context(nc.Block())

            def run(ename, eng):
                waited = {}
                for o in self.ops[ename]:
                    need = {}
                    for d in o.deps:
                        if d.dma:
                            key = ("d", d.slot)
                            val = d.val
                        else:
                            key = ("e", d.eng)
                            val = d.sig
                        if need.get(key, 0) < val:
                            need[key] = val
                    for key, val in need.items():
                        if waited.get(key, 0) >= val:
                            continue
                        waited[key] = val
                        sem = ssem[key[1]] if key[0] == "d" else esem[key[1]]
                        eng.wait_ge(sem, val)
                        stats["waits"] += 1
                    if o.fn is None:
                        continue
                    ins = getattr(eng, o.fn[0])(*o.fn[1], **o.fn[2])
                    stats["ins"] += 1
                    if o.dma:
                        ins.then_inc(ssem[o.slot], 16)
                    elif o.needed:
                        ins.then_inc(esem[ename], 1)
                if ename == "sp":
                    for k in final_slots:
                        eng.wait_ge(ssem[k], 16 * self.slot_cnt[k])

            @block.tensor
            def _(e):
                run("pe", e)

            @block.scalar
            def _(e):
                run("act", e)

            @block.vector
            def _(e):
                run("dve", e)

            @block.gpsimd
            def _(e):
                run("pool", e)

            @block.sync
            def _(e):
                run("sp", e)
        return stats


_GB = [0, 32, 252, 264, 268, 284, 348, 358, 380, 512, 520, 528, 536, 544, 568, 716, 726, 788, 800, 804, 808, 820, 1024, 1048, 1056, 1064, 1072]


def gran(a, b):
    return [i for i in range(len(_GB) - 1) if _GB[i] < b and _GB[i + 1] > a]


def T(name, k, a, b):
    return [(name, k, g) for g in gran(a, b)]


P_NG = 0
P_FG = 96
P_ACW = 112
P_ACB = 360
P_ALG = 368
P_ALB = 376
P_BCW = 384
P_CBI = 408
P_CLG = 440
P_CLB = 456
P_N = 472
B_W00 = 0
B_BS0 = 8
B_BS = 16
B_N = 16 + 1024


def build_program(stop=None):
    nc = bass.Bass("TRN2", target_bir_lowering=False)

    def din(name, shape):
        return nc.dram_tensor(name, shape, F32, kind="ExternalInput").ap()

    def dout(name, shape):
        return nc.dram_tensor(name, shape, F32, kind="ExternalOutput").ap()

    xT = din("xT", [D, W])
    sa = din("sa", [1024, 30, NSM])
    sb = din("sb", [1024, 2, NSM])
    w1 = din("w1", [4, D, DFF])
    w3 = din("w3", [4, D, DFF])
    w2 = din("w2", [4, DFF, D])
    abin = din("abin", [D, 5120])
    about = din("about", [D, D])
    cin = din("cin", [D, 4096])
    cout = din("cout", [D, D])
    prm = din("prm", [128, P_N])
    wst = din("wst", [128, 1024])
    brow = din("brow", [1, B_N])
    yT = dout("yT", [D, TOUT])
    caP = dout("caP", [1024, 30])
    cbP = dout("cbP", [1024, 2])
    caS = dout("caS", [1024, 30, NSM])
    cbS = dout("cbS", [1024, 2, NSM])
    vS = dout("vS", [D, NSM])

    def kview(ap2d):
        return ap2d.rearrange("(c p) n -> p c n", p=128)

    w1v = [kview(w1[f]) for f in range(4)]
    w3v = [kview(w3[f]) for f in range(4)]
    w2v = [kview(w2[f]) for f in range(4)]
    abinv = kview(abin)
    aboutv = kview(about)
    cinv = kview(cin)
    coutv = kview(cout)
    xTv = kview(xT)
    yTv = kview(yT)
    vSv = kview(vS)

    S = Sched()
    AW = 53000
    arena = nc.alloc_sbuf_tensor("arena", [128, AW], F32)
    ps = [nc.alloc_psum_tensor("ps%d" % i, [128, 512], F32) for i in range(8)]
    cur = [0]

    def alloc(nwords):
        o = cur[0]
        cur[0] += (nwords + 7) // 8 * 8
        assert cur[0] <= AW, ("arena overflow", cur[0])
        return o

    def vf32(off, n):
        return arena[:, off:off + n]

    def vbf(off, nwords):
        return arena[:, off:off + nwords].bitcast(BF16)

    o_X = alloc(16 * W)
    X = vf32(o_X, 16 * W).rearrange("p (c t) -> p c t", t=W)
    o_HN = alloc(8 * W)
    HN = vbf(o_HN, 8 * W).rearrange("p (c t) -> p c t", t=W)
    WS = []
    for i in range(4):
        o = alloc(2048)
        WS.append(vbf(o, 2048).rearrange("p (c n) -> p c n", n=256))
    PRM = vf32(alloc(P_N), P_N)
    o_BROW = cur[0]
    BROW = vf32(alloc(1072), 1072)[:, 0:B_N]
    ONES = vf32(alloc(128), 128)
    IDF = vf32(alloc(128), 128)
    IDB = vbf(alloc(64), 64)
    RSTD = vf32(alloc(W), W)
    o_SQ = alloc(1024)
    SQ = [vf32(o_SQ, 512), vf32(o_SQ + 512, 512)]
    WSTB = vbf(alloc(512), 512).rearrange("p (h t) -> p h t", t=128)
    WSSB = vbf(alloc(64), 64).rearrange("p (h t) -> p h t", t=16)
    BSBS = vf32(alloc(128), 128).rearrange("p (h t) -> p h t", t=16)
    VS = vf32(alloc(256), 256).rearrange("p (c b) -> p c b", b=16)
    o_FULLS = cur[0]
    FULLS = vf32(alloc(31 * 16), 31 * 16).rearrange("p (k b) -> p k b", b=16)
    FULLSB = vf32(alloc(48), 48).rearrange("p (k b) -> p k b", b=16)
    ACARRY = vf32(alloc(256), 256).rearrange("p (c t) -> p c t", t=32)
    CHCARRY = vf32(alloc(256), 256).rearrange("p (c t) -> p c t", t=32)
    EPSC = vf32(alloc(8), 8)
    prm_eps = EPSC[:, 0:1]
    o_R = cur[0]
    RW = AW - o_R
    LNB = {"mu": vf32(o_BROW, 1072)[:, 0:536], "rs": vf32(o_BROW, 1072)[:, 536:1072], "name": "lnb0"}
    LNB1 = {"mu": vf32(o_FULLS, 1056)[:, 0:528], "rs": vf32(o_FULLS, 1056)[:, 528:1056], "name": "lnb1"}
    LNC = [LNB]

    bank_i = [0]

    def nb():
        b = bank_i[0] % 6
        bank_i[0] += 1
        return b

    def prm_col(c):
        return PRM[:, c:c + 1]

    slot_i = [0]

    def load_w(src_ap, nk):
        sl = slot_i[0] % 4
        dst = WS[sl][:, 0:nk, :]
        xd = [S.last_dma[("x", 2, 0)], S.last_dma[("x", 2, 1)]] if (slot_i[0] == 2 and ("x", 2, 0) in S.last_dma) else []
        slot_i[0] += 1
        S.add("pool", CALL("dma_start", out=dst, in_=src_ap), writes=[("ws", sl)], slot=("ws", sl), extra_deps=xd)
        return sl

    defer = [None, {}]

    def mm_group(bank, tn, sl, m, nk, rhs_of, rhs_toks, stat_tile=None):
        pst = ps[bank][:, 0:tn]
        for k in range(nk):
            lhsT = WS[sl][:, k, m * 128:(m + 1) * 128]
            rhs = rhs_of(k)
            S.add("pe", CALL("matmul", pst, lhsT=lhsT, rhs=rhs, start=(k == 0), stop=(k == nk - 1)),
                  reads=[("ws", sl)] + rhs_toks(k), writes=[("ps", bank)])
        if defer[0] is not None:
            t = defer[0]
            defer[0] = None
            rstd_begin(t)
        if stat_tile is not None and stat_tile[0] in defer[1]:
            del defer[1][stat_tile[0]]
            rstd_begin([stat_tile])

    def rstd_defer(tiles, per_tile=False):
        if per_tile:
            defer[1] = {t0: tn for (t0, tn) in tiles}
        else:
            defer[0] = tiles

    S.add("sp", CALL("dma_start", out=PRM, in_=prm[:, :]), writes=["prm"], slot="prm")
    S.add("sp", CALL("dma_start", out=BROW, in_=brow.partition_broadcast(128)), writes=["brow"], slot="brow")
    for ti, (t0, tn) in enumerate([(0, 358), (358, 358), (716, 356)]):
        wr = []
        for k in range(16):
            wr += T("x", k, t0, t0 + tn)
        for hq, (qn, k0) in enumerate((("sp", 0), ("act", 8))):
            wrh = []
            for k in range(k0, k0 + 8):
                wrh += T("x", k, t0, t0 + tn)
            S.add(qn, CALL("dma_start", out=X[:, k0:k0 + 8, t0:t0 + tn], in_=xTv[:, k0:k0 + 8, t0:t0 + tn]), writes=wrh,
                  slot=("x", ti, hq))
    S.add("dve", CALL("memset", ONES, 1.0), writes=["ones"])
    S.add("dve", CALL("memset", IDF, 1.0), writes=["idf"])
    S.add("pool", CALL("affine_select", out=IDF, in_=IDF, pattern=[[1, 128]], compare_op=ALU.is_equal, fill=0.0,
                                            base=0, channel_multiplier=-1), reads=["idf"], writes=["idf"])
    S.add("dve", CALL("tensor_copy", out=IDB, in_=IDF), reads=["idf"], writes=["idb"])
    def colsum(bank, tn, nk, src_of, src_toks, square):
        for k in range(nk):
            if square:
                sq = SQ[k % 2]
                src = src_of(k)
                sqr = sq[:, 0:tn]
                S.add("act", CALL("activation", out=sqr, in_=src, func=AF.Square),
                      reads=src_toks(k), writes=[("sq", k % 2)])
                S.add("pe", CALL("matmul", ps[bank][:, 0:tn], lhsT=ONES, rhs=sqr, start=(k == 0), stop=(k == nk - 1)),
                      reads=["ones", ("sq", k % 2)], writes=[("ps", bank)])
            else:
                S.add("pe", CALL("matmul", ps[bank][:, 0:tn], lhsT=ONES, rhs=src_of(k), start=(k == 0), stop=(k == nk - 1)),
                      reads=["ones"] + src_toks(k), writes=[("ps", bank)])

    def rstd_from(src, src_toks, dst, dst_toks, inv_n):
        S.add("act", CALL("activation", out=dst, in_=src, func=AF.Sqrt, bias=prm_eps, scale=inv_n),
              reads=src_toks + ["eps"], writes=dst_toks)
        S.add("dve", CALL("reciprocal", out=dst, in_=dst), reads=dst_toks, writes=dst_toks)

    S.add("dve", CALL("memset", EPSC, EPS), writes=["eps"])

    sq_i = [0]
    pend = []

    def zero_rstd(a, b):
        S.add("dve", CALL("memset", RSTD[:, a:b], 0.0), writes=T("rstd", 0, a, b))

    def flush_one():
        q, g0, n = pend.pop(0)
        S.add("dve", CALL("tensor_tensor", out=RSTD[:, g0:g0 + n], in0=RSTD[:, g0:g0 + n], in1=SQ[q][:, 0:n], op=ALU.add),
              reads=[("sq", q)] + T("rstd", 0, g0, g0 + n), writes=T("rstd", 0, g0, g0 + n))

    def flush_all():
        while pend:
            flush_one()

    def accum_sq(src, src_toks, g0, n):
        q = sq_i[0] % 2
        sq_i[0] += 1
        S.add("act", CALL("activation", out=SQ[q][:, 0:n], in_=src, func=AF.Square), reads=src_toks, writes=[("sq", q)])
        pend.append((q, g0, n))
        if len(pend) > 1:
            flush_one()

    def rmsnorm(gcol, tiles, out_of, out_toks, fused):
        if not fused:
            zero_rstd(tiles[0][0], W)
            for (t0, tn) in tiles:
                for k in range(16):
                    accum_sq(X[:, k, t0:t0 + tn], T("x", k, t0, t0 + tn), t0, tn)
            flush_all()
        for (t0, tn) in tiles:
            bank = nb()
            S.add("pe", CALL("matmul", ps[bank][:, 0:tn], lhsT=ONES, rhs=RSTD[:, t0:t0 + tn], start=True, stop=True),
                  reads=["ones"] + T("rstd", 0, t0, t0 + tn), writes=[("ps", bank)])
            rstd_from(ps[bank][:, 0:tn], [("ps", bank)], RSTD[:, t0:t0 + tn], T("rstd", 0, t0, t0 + tn), 1.0 / D)
            for k in range(16):
                o = out_of(k, t0, tn)
                S.add("dve", CALL("scalar_tensor_tensor", out=o, in0=X[:, k, t0:t0 + tn], scalar=prm_col(gcol + k),
                                  in1=RSTD[:, t0:t0 + tn], op0=ALU.mult, op1=ALU.mult),
                      reads=T("x", k, t0, t0 + tn) + T("rstd", 0, t0, t0 + tn) + ["prm"], writes=out_toks(k, t0, tn))

    def norm_to_hn(gcol, tiles, fused):
        rmsnorm(gcol, tiles, lambda k, t0, tn: HN[:, k, t0:t0 + tn], lambda k, t0, tn: T("hn", k, t0, t0 + tn), fused)

    G = vbf(o_R, 8 * W).rearrange("p (c t) -> p c t", t=W)
    assert 8 * W <= RW

    def rstd_begin(tiles):
        for i, (t0, tn) in enumerate(tiles):
            bank = 6 + i % 2
            S.add("pe", CALL("matmul", ps[bank][:, 0:tn], lhsT=ONES, rhs=RSTD[:, t0:t0 + tn], start=True, stop=True),
                  reads=["ones"] + T("rstd", 0, t0, t0 + tn), writes=[("ps", bank)])
            rstd_from(ps[bank][:, 0:tn], [("ps", bank)], RSTD[:, t0:t0 + tn], T("rstd", 0, t0, t0 + tn), 1.0 / D)

    def hn_xg(oc, g0, n, gcol):
        S.add("act", CALL("mul", out=HN[:, oc, g0:g0 + n], in_=X[:, oc, g0:g0 + n], mul=prm_col(gcol + oc)),
              reads=T("x", oc, g0, g0 + n) + ["prm"], writes=T("hn", oc, g0, g0 + n))

    def post_mixer_epilogue(norm_idx, half, oc, gt, tn):
        hn_xg(oc, gt, tn, P_NG + norm_idx * 16)
        if half == 1:
            accum_sq(X[:, oc, gt:gt + tn], T("x", oc, gt, gt + tn), gt, tn)

    def setup_spatial():
        WSTF = vf32(o_FULLS, 1024)
        S.add("sp", CALL("dma_start", out=WSTF, in_=wst[:, :]), writes=["wstf"], slot="wstf")
        S.add("pool", CALL("affine_select", out=WSTF.rearrange("p (h t) -> p h t", t=128), in_=WSTF.rearrange("p (h t) -> p h t", t=128),
                                                pattern=[[0, 8], [1, 128]], compare_op=ALU.is_ge, fill=0.0, base=0,
                                                channel_multiplier=-1), reads=["wstf"], writes=["wstf"])
        S.add("dve", CALL("tensor_copy", out=WSTB.rearrange("p h t -> p (h t)"), in_=WSTF), reads=["wstf"], writes=["wstb"])
        for h in range(8):
            S.add("dve", CALL("tensor_scalar", out=WSSB[0:16, h, :], in0=IDF[0:16, 0:16], scalar1=BROW[0:16, B_W00 + h:B_W00 + h + 1],
                                                        scalar2=None, op0=ALU.mult), reads=["idf", "brow"], writes=[("wssb", h)])
            S.add("dve", CALL("tensor_scalar", out=BSBS[:, h, :], in0=ONES[:, 0:16], scalar1=BROW[:, B_BS0 + h:B_BS0 + h + 1],
                                                        scalar2=None, op0=ALU.mult), reads=["ones", "brow"], writes=[("bsbs", h)])

    def ffn(f, tiles, next_gcol=None, hook=None):
        for (h0, hn) in ((0, 16), (16, 16), (32, 12)):
            for j in range(hn // 2):
                n0 = (h0 + 2 * j) * 128
                s1 = load_w(w1v[f][:, :, n0:n0 + 256], 16)
                s3 = load_w(w3v[f][:, :, n0:n0 + 256], 16)
                for m in range(2):
                    gc = 2 * j + m
                    for (t0, tn) in tiles:
                        bA = nb()
                        mm_group(bA, tn, s1, m, 16, lambda k: HN[:, k, t0:t0 + tn], lambda k: T("hn", k, t0, t0 + tn),
                                 stat_tile=(t0, tn))
                        bB = nb()
                        mm_group(bB, tn, s3, m, 16, lambda k: HN[:, k, t0:t0 + tn], lambda k: T("hn", k, t0, t0 + tn))
                        q = sq_i[0] % 2
                        sq_i[0] += 1
                        rt = T("rstd", 0, t0, t0 + tn)
                        S.add("dve", CALL("tensor_tensor", out=SQ[q][:, 0:tn], in0=ps[bA][:, 0:tn], in1=RSTD[:, t0:t0 + tn], op=ALU.mult),
                              reads=[("ps", bA)] + rt, writes=[("sq", q)])
                        S.add("act", CALL("activation", out=SQ[q][:, 0:tn], in_=SQ[q][:, 0:tn], func=AF.Silu),
                              reads=[("sq", q)], writes=[("sq", q)])
                        S.add("dve", CALL("tensor_tensor", out=SQ[q][:, 0:tn], in0=SQ[q][:, 0:tn], in1=RSTD[:, t0:t0 + tn], op=ALU.mult),
                              reads=[("sq", q)] + rt, writes=[("sq", q)])
                        S.add("dve", CALL("tensor_tensor",
                            out=G[:, gc, t0:t0 + tn], in0=SQ[q][:, 0:tn], in1=ps[bB][:, 0:tn], op=ALU.mult),
                            reads=[("sq", q), ("ps", bB)], writes=T("g", gc, t0, t0 + tn))
            if h0 == 0 and hook is not None:
                hook()
            if h0 == 32:
                zero_rstd(tiles[0][0], W)
            for nb2 in range(8):
                s = load_w(w2v[f][:, h0:h0 + hn, nb2 * 256:(nb2 + 1) * 256], hn)
                for m in range(2):
                    oc = nb2 * 2 + m
                    for (t0, tn) in tiles:
                        b = nb()
                        mm_group(b, tn, s, m, hn, lambda k: G[:, k, t0:t0 + tn], lambda k: T("g", k, t0, t0 + tn))
                        S.add("dve", CALL("scalar_tensor_tensor",
                            out=X[:, oc, t0:t0 + tn], in0=ps[b][:, 0:tn], scalar=0.5, in1=X[:, oc, t0:t0 + tn],
                            op0=ALU.mult, op1=ALU.add),
                            reads=[("ps", b)] + T("x", oc, t0, t0 + tn), writes=T("x", oc, t0, t0 + tn))
                        if h0 == 32:
                            if next_gcol is not None:
                                hn_xg(oc, t0, tn, next_gcol)
                            accum_sq(X[:, oc, t0:t0 + tn], T("x", oc, t0, t0 + tn), t0, tn)
            if h0 == 32:
                flush_all()

    def mu_toks(t0, tn):
        return T(LNC[0]["name"], 0, t0, t0 + tn)

    def rs_toks(t0, tn):
        return T(LNC[0]["name"], 1, t0, t0 + tn)

    def ln_zero():
        S.add("dve", CALL("memset", LNC[0]["mu"], 0.0), writes=T(LNC[0]["name"], 0, 0, 536))
        S.add("dve", CALL("memset", LNC[0]["rs"], 0.0), writes=T(LNC[0]["name"], 1, 0, 536))

    def ln_flush():
        pass

    def ln_accum(src, src_toks, t0, tn):
        MU, RS = LNC[0]["mu"], LNC[0]["rs"]
        S.add("dve", CALL("tensor_tensor", out=MU[:, t0:t0 + tn], in0=MU[:, t0:t0 + tn], in1=src, op=ALU.add),
              reads=src_toks + mu_toks(t0, tn), writes=mu_toks(t0, tn))
        q = sq_i[0] % 2
        sq_i[0] += 1
        S.add("act", CALL("activation", out=SQ[q][:, 0:tn], in_=src, func=AF.Square), reads=src_toks, writes=[("sq", q)])
        S.add("dve", CALL("tensor_tensor", out=RS[:, t0:t0 + tn], in0=RS[:, t0:t0 + tn], in1=SQ[q][:, 0:tn], op=ALU.add),
              reads=[("sq", q)] + rs_toks(t0, tn), writes=rs_toks(t0, tn))

    def ln_stats(nch, tiles):
        inv = 1.0 / (nch * 128)
        MU, RS = LNC[0]["mu"], LNC[0]["rs"]
        for (t0, tn) in tiles:
            b1 = nb()
            S.add("pe", CALL("matmul", ps[b1][:, 0:tn], lhsT=ONES, rhs=MU[:, t0:t0 + tn], start=True, stop=True),
                  reads=["ones"] + mu_toks(t0, tn), writes=[("ps", b1)])
            b2 = nb()
            S.add("pe", CALL("matmul", ps[b2][:, 0:tn], lhsT=ONES, rhs=RS[:, t0:t0 + tn], start=True, stop=True),
                  reads=["ones"] + rs_toks(t0, tn), writes=[("ps", b2)])
            S.add("dve", CALL("tensor_scalar", out=MU[:, t0:t0 + tn], in0=ps[b1][:, 0:tn], scalar1=inv, scalar2=None, op0=ALU.mult),
                  reads=[("ps", b1)], writes=mu_toks(t0, tn))
            S.add("dve", CALL("tensor_tensor", out=SQ[0][:, 0:tn], in0=MU[:, t0:t0 + tn], in1=MU[:, t0:t0 + tn], op=ALU.mult),
                  reads=mu_toks(t0, tn), writes=[("sq", 0)])
            S.add("dve", CALL("scalar_tensor_tensor", out=RS[:, t0:t0 + tn], in0=ps[b2][:, 0:tn], scalar=inv, in1=SQ[0][:, 0:tn],
                              op0=ALU.mult, op1=ALU.subtract),
                  reads=[("ps", b2), ("sq", 0)], writes=rs_toks(t0, tn))
            rstd_from(RS[:, t0:t0 + tn], rs_toks(t0, tn), RS[:, t0:t0 + tn], rs_toks(t0, tn), 1.0)

    def ln_apply(buf, name, c, tiles, finish):
        MU, RS = LNC[0]["mu"], LNC[0]["rs"]
        for (t0, tn) in tiles:
            q = sq_i[0] % 2
            sq_i[0] += 1
            S.add("dve", CALL("tensor_tensor", out=SQ[q][:, 0:tn], in0=buf[:, c, t0:t0 + tn], in1=MU[:, t0:t0 + tn], op=ALU.subtract),
                  reads=T(name, c, t0, t0 + tn) + mu_toks(t0, tn), writes=[("sq", q)])
            S.add("dve", CALL("tensor_tensor", out=SQ[q][:, 0:tn], in0=SQ[q][:, 0:tn], in1=RS[:, t0:t0 + tn], op=ALU.mult),
                  reads=[("sq", q)] + rs_toks(t0, tn), writes=[("sq", q)])
            finish(c, t0, tn, SQ[q][:, 0:tn], ("sq", q))

    def layernorm_cols(buf, name, nch, tiles, finish):
        ln_stats(nch, tiles)
        for c in range(nch):
            ln_apply(buf, name, c, tiles, finish)

    def mixer0():
        LNC[0] = LNB
        cur_r = [o_R]

        def ralloc(n):
            o = cur_r[0]
            cur_r[0] += (n + 7) // 8 * 8
            assert cur_r[0] <= AW, "R overflow mixer0"
            return o

        AC = vf32(ralloc(8 * 536), 8 * 536).rearrange("p (c t) -> p c t", t=536)
        AO = vbf(ralloc(4 * 536), 4 * 536).rearrange("p (c t) -> p c t", t=536)
        BO = vbf(ralloc(4 * 536), 4 * 536).rearrange("p (c t) -> p c t", t=536)
        o_A = cur_r[0]
        AFB = vf32(ralloc(568), 568)
        AFB16s = [vbf(ralloc(284), 284) for _ in range(2)]
        DG = vbf(ralloc(31 * 64), 31 * 64).rearrange("p (k m) -> p k m", m=128)
        FULLS16s = [vbf(ralloc(248), 248).rearrange("p (k b) -> p k b", b=16) for _ in range(2)]
        cur_r[0] = o_A
        CHF = vf32(ralloc(568), 568)
        BG = vf32(ralloc(536), 536)
        ACCB = vf32(ralloc(536), 536)

        for half in range(2):
            g0 = 536 * half
            tl_in = [(0, 268), (268, 268)]
            if half == 0:
                tl_out = [(32, 252), (284, 252)]
                p0, pn = 32, 504
            else:
                tl_out = [(0, 268), (268, 268)]
                p0, pn = 0, 520
            ln_zero()
            if half == 0:
                S.add("dve", CALL("memset", AFB[:, 0:32], 0.0), writes=[("afb", 0)])
            slots_a = {}

            def A1(c):
                cb, m = c // 2, c % 2
                if m == 0:
                    slots_a[cb] = (load_w(abinv[:, :, cb * 256:(cb + 1) * 256], 16),
                                   load_w(abinv[:, :, 1024 + cb * 256:1024 + (cb + 1) * 256], 16))
                s_pa, s_ga = slots_a[cb]
                AFB16 = AFB16s[c % 2]
                FULLS16 = FULLS16s[c % 2]
                if half == 1:
                    S.add("act", CALL("copy", out=AFB[:, 0:32], in_=ACARRY[:, c, :]),
                          reads=[("acarry", c)], writes=[("afb", 0)])
                for (t0, tn) in tl_in:
                    gt = g0 + t0
                    b1 = nb()
                    mm_group(b1, tn, s_pa, m, 16, lambda k: HN[:, k, gt:gt + tn], lambda k: T("hn", k, gt, gt + tn))
                    b2 = nb()
                    mm_group(b2, tn, s_ga, m, 16, lambda k: HN[:, k, gt:gt + tn], lambda k: T("hn", k, gt, gt + tn))
                    tq = 0 if t0 == 0 else 1
                    rt = T("rstd", 0, gt, gt + tn)
                    S.add("dve", CALL("tensor_tensor", out=SQ[tq][:, 0:tn], in0=ps[b2][:, 0:tn], in1=RSTD[:, gt:gt + tn], op=ALU.mult),
                          reads=[("ps", b2)] + rt, writes=[("sq", tq)])
                    S.add("act", CALL("activation", out=SQ[tq][:, 0:tn], in_=SQ[tq][:, 0:tn], func=AF.Sigmoid),
                          reads=[("sq", tq)], writes=[("sq", tq)])
                    S.add("dve", CALL("tensor_tensor", out=SQ[tq][:, 0:tn], in0=SQ[tq][:, 0:tn], in1=RSTD[:, gt:gt + tn], op=ALU.mult),
                          reads=[("sq", tq)] + rt, writes=[("sq", tq)])
                    S.add("dve", CALL("tensor_tensor", out=AFB[:, 32 + t0:32 + t0 + tn], in0=SQ[tq][:, 0:tn],
                                      in1=ps[b1][:, 0:tn], op=ALU.mult),
                          reads=[("ps", b1), ("sq", tq)], writes=[("afb", 1 + t0)])
                afb_all = [("afb", 0), ("afb", 1), ("afb", 269)]
                if half == 0:
                    S.add("act", CALL("copy", out=ACARRY[:, c, :], in_=AFB[:, 536:568]),
                          reads=afb_all, writes=[("acarry", c)])
                S.add("act", CALL("copy", out=AFB16, in_=AFB), reads=afb_all, writes=[("afb16", c % 2)])
                if half == 1:
                    S.add("sp", CALL("dma_start", out=caP[c * 128:(c + 1) * 128, :], in_=AFB[:, 32 + 490:32 + 520]),
                          reads=afb_all, writes=[("caP", c)], slot=("caP", c % 2))
                    S.add("sp", CALL("dma_start", out=FULLS[:, 0:30, :], in_=sa[c * 128:(c + 1) * 128, :, :]),
                          writes=[("fulls", 0)], slot=("fulls", 0))
                    S.add("act", CALL("copy", out=FULLS[:, 30, :], in_=AFB[:, 32 + 520:32 + 536]),
                          reads=afb_all, writes=[("fulls", 1)])
                    S.add("act", CALL("copy", out=FULLS16, in_=FULLS), reads=[("fulls", 0), ("fulls", 1)], writes=[("fulls16", c % 2)])
                    S.add("sp", CALL("dma_start", out=caS[c * 128:(c + 1) * 128, :, :], in_=FULLS[:, 1:31, :]),
                          reads=[("fulls", 0), ("fulls", 1)], writes=[("caS", c)], slot=("caS", c % 2))

            def DGB(c):
                for k in range(31):
                    if k % 2 == 0:
                        S.add("dve", CALL("tensor_scalar", out=DG[:, k, :], in0=IDF, scalar1=prm_col(P_ACW + c * 31 + k), scalar2=None,
                                          op0=ALU.mult), reads=["idf", "prm"], writes=[("dg", k)])
                    else:
                        S.add("act", CALL("mul", out=DG[:, k, :], in_=IDF, mul=prm_col(P_ACW + c * 31 + k)),
                              reads=["idf", "prm"], writes=[("dg", k)])

            def A2(c):
                AFB16 = AFB16s[c % 2]
                FULLS16 = FULLS16s[c % 2]
                for ti, (t0, tn) in enumerate(tl_out):
                    bank = nb()
                    pn_t = tn
                    if half == 1 and ti == 1:
                        pn_t = tn - 16
                    for k in range(31):
                        S.add("pe", CALL("matmul", ps[bank][:, 0:pn_t], lhsT=DG[:, k, :], rhs=AFB16[:, 2 + t0 + k:2 + t0 + k + pn_t],
                                         start=(k == 0), stop=(k == 30), skip_group_check=True),
                              reads=[("dg", k), ("afb16", c % 2)], writes=[("ps", bank)])
                    if pn_t != tn:
                        for k in range(31):
                            S.add("pe", CALL("matmul", ps[bank][:, pn_t:tn], lhsT=DG[:, k, :], rhs=FULLS16[:, k, :],
                                             start=(k == 0), stop=(k == 30), skip_group_check=True),
                                  reads=[("dg", k), ("fulls16", c % 2)], writes=[("ps", bank)])
                    S.add("act", CALL("activation", out=AC[:, c, t0:t0 + tn], in_=ps[bank][:, 0:tn], func=AF.Identity,
                                      bias=prm_col(P_ACB + c), scale=1.0),
                          reads=[("ps", bank), "prm"], writes=T("ac", c, t0, t0 + tn))
                    ln_accum(AC[:, c, t0:t0 + tn], T("ac", c, t0, t0 + tn), t0, tn)

            A1(0)
            DGB(0)
            for c in range(8):
                if c + 1 < 8:
                    A1(c + 1)
                A2(c)
                if c + 1 < 8:
                    DGB(c + 1)
            def fin_a(c, t0, tn, tmp, tmptok):
                S.add("act", CALL("activation",
                    out=AO[:, c, t0:t0 + tn], in_=tmp, func=AF.Silu, bias=prm_col(P_ALB + c), scale=prm_col(P_ALG + c)),
                    reads=[tmptok, "prm"], writes=T("ao", c, t0, t0 + tn))
            S.barrier(dma_slots=[("caP", 0), ("caP", 1), ("cbP", 0), ("cbP", 1)])
            ln_stats(8, tl_out)
            for cb in range(4):
                s_bg = load_w(abinv[:, :, 2048 + cb * 256:2048 + (cb + 1) * 256], 16)
                s_cg = load_w(abinv[:, :, 3072 + cb * 256:3072 + (cb + 1) * 256], 16)
                s_hb = load_w(abinv[:, :, 4096 + cb * 256:4096 + (cb + 1) * 256], 16)
                for m in range(2):
                    c = cb * 2 + m
                    if half == 1:
                        S.add("act", CALL("copy", out=CHF[:, 0:32], in_=CHCARRY[:, c, :]),
                              reads=[("chcarry", c)], writes=[("chf", 0)])
                    for (t0, tn) in tl_in:
                        gt = g0 + t0
                        b1 = nb()
                        mm_group(b1, tn, s_bg, m, 16, lambda k: HN[:, k, gt:gt + tn], lambda k: T("hn", k, gt, gt + tn))
                        b2 = nb()
                        mm_group(b2, tn, s_cg, m, 16, lambda k: HN[:, k, gt:gt + tn], lambda k: T("hn", k, gt, gt + tn))
                        b3 = nb()
                        mm_group(b3, tn, s_hb, m, 16, lambda k: HN[:, k, gt:gt + tn], lambda k: T("hn", k, gt, gt + tn))
                        rt = T("rstd", 0, gt, gt + tn)
                        S.add("dve", CALL("tensor_tensor", out=BG[:, t0:t0 + tn], in0=ps[b1][:, 0:tn], in1=RSTD[:, gt:gt + tn], op=ALU.mult),
                              reads=[("ps", b1)] + rt, writes=[("bg", t0)])
                        tq = 0 if t0 == 0 else 1
                        S.add("dve", CALL("tensor_tensor", out=SQ[tq][:, 0:tn], in0=ps[b3][:, 0:tn], in1=RSTD[:, gt:gt + tn], op=ALU.mult),
                              reads=[("ps", b3)] + rt, writes=[("sq", tq)])
                        S.add("dve", CALL("tensor_tensor", out=CHF[:, 32 + t0:32 + t0 + tn], in0=ps[b2][:, 0:tn], in1=RSTD[:, gt:gt + tn],
                                          op=ALU.mult),
                              reads=[("ps", b2)] + rt, writes=[("chf", 1 + t0)])
                        S.add("dve", CALL("tensor_tensor", out=CHF[:, 32 + t0:32 + t0 + tn], in0=CHF[:, 32 + t0:32 + t0 + tn],
                                          in1=SQ[tq][:, 0:tn], op=ALU.mult),
                              reads=[("sq", tq), ("chf", 1 + t0)], writes=[("chf", 1 + t0)])
                    chf_all = [("chf", 0), ("chf", 1), ("chf", 269)]
                    if half == 0:
                        S.add("act", CALL("copy", out=CHCARRY[:, c, :], in_=CHF[:, 536:568]),
                              reads=chf_all, writes=[("chcarry", c)])
                    acc = ACCB[:, p0:p0 + pn]
                    for k in range(3):
                        src = CHF[:, 30 + p0 + k:30 + p0 + k + pn]
                        wk = prm_col(P_BCW + c * 3 + k)
                        if k == 0:
                            S.add("dve", CALL("tensor_scalar", out=acc, in0=src, scalar1=wk, scalar2=None, op0=ALU.mult),
                                  reads=chf_all + ["prm"], writes=[("accb", 0)])
                        else:
                            S.add("dve", CALL("scalar_tensor_tensor",
                                out=acc, in0=src, scalar=wk, in1=acc, op0=ALU.mult, op1=ALU.add),
                                reads=chf_all + [("accb", 0), "prm"], writes=[("accb", 0)])
                    bread = [("accb", 0)]
                    if half == 1:
                        S.add("sp", CALL("dma_start", out=cbP[c * 128:(c + 1) * 128, :], in_=CHF[:, 32 + 518:32 + 520]),
                              reads=chf_all, writes=[("cbP", c)], slot=("cbP", c % 2))
                        S.add("sp", CALL("dma_start", out=FULLSB[:, 0:2, :], in_=sb[c * 128:(c + 1) * 128, :, :]),
                              writes=[("fullsb", 0)], slot=("fullsb", 0))
                        S.add("act", CALL("copy", out=FULLSB[:, 2, :], in_=CHF[:, 32 + 520:32 + 536]),
                              reads=chf_all, writes=[("fullsb", 1)])
                        accs = ACCB[:, 520:536]
                        for k in range(3):
                            wk = prm_col(P_BCW + c * 3 + k)
                            src = FULLSB[:, k, :]
                            if k == 0:
                                S.add("dve", CALL("tensor_scalar", out=accs, in0=src, scalar1=wk, scalar2=None, op0=ALU.mult),
                                      reads=[("fullsb", 0), ("fullsb", 1), "prm"], writes=[("accb", 1)])
                            else:
                                S.add("dve", CALL("scalar_tensor_tensor",
                                    out=accs, in0=src, scalar=wk, in1=accs, op0=ALU.mult, op1=ALU.add),
                                    reads=[("fullsb", 0), ("fullsb", 1), ("accb", 1), "prm"], writes=[("accb", 1)])
                        S.add("sp", CALL("dma_start", out=cbS[c * 128:(c + 1) * 128, :, :], in_=FULLSB[:, 1:3, :]),
                              reads=[("fullsb", 0), ("fullsb", 1)], writes=[("cbS", c)], slot=("cbS", c % 2))
                        bread = [("accb", 0), ("accb", 1)]
                    o0 = tl_out[0][0]
                    on = tl_out[-1][0] + tl_out[-1][1] - o0
                    S.add("dve", CALL("tensor_tensor", out=BO[:, c, o0:o0 + on], in0=ACCB[:, o0:o0 + on],
                                                                              in1=BG[:, o0:o0 + on], op=ALU.mult),
                          reads=bread + [("bg", 0), ("bg", 268)], writes=T("bo", c, o0, o0 + on))
                    ln_apply(AC, "ac", c, tl_out, fin_a)
            if half == 1:
                half0_sq(32, 536)
                zero_rstd(536, W)
            for nb2 in range(8):
                s = load_w(aboutv[:, :, nb2 * 256:(nb2 + 1) * 256], 16)
                for m in range(2):
                    oc = nb2 * 2 + m
                    for (t0, tn) in tl_out:
                        b = nb()
                        mm_group(b, tn, s, m, 16,
                                 lambda k: (AO[:, k, t0:t0 + tn] if k < 8 else BO[:, k - 8, t0:t0 + tn]),
                                 lambda k: (T("ao", k, t0, t0 + tn) if k < 8 else T("bo", k - 8, t0, t0 + tn)))
                        gt = g0 + t0
                        S.add("dve", CALL("tensor_tensor",
                            out=X[:, oc, gt:gt + tn], in0=ps[b][:, 0:tn], in1=X[:, oc, gt:gt + tn], op=ALU.add),
                            reads=[("ps", b)] + T("x", oc, gt, gt + tn), writes=T("x", oc, gt, gt + tn))
                        post_mixer_epilogue(2, half, oc, gt, tn)
            if half == 1:
                flush_all()
            S.barrier(dma_slots=[("caP", 0), ("caP", 1), ("cbP", 0), ("cbP", 1)])
        S.add("sp", CALL("dma_start", out=BROW, in_=brow.partition_broadcast(128)),
              writes=["brow"] + T("lnb0", 0, 0, 536) + T("lnb0", 1, 0, 536), slot="brow")

    def mixer1():
        LNC[0] = LNB1
        S.barrier(dma_slots=[("caS", 0), ("caS", 1), ("cbS", 0), ("cbS", 1), ("fulls", 0), ("fullsb", 0)])
        o_V = o_R
        V = vf32(o_V, 16 * 528).rearrange("p (c t) -> p c t", t=528)
        o_VNB = o_R + 16 * 528
        VNB = vbf(o_VNB, 8 * 528).rearrange("p (c t) -> p c t", t=528)
        assert o_VNB + 8 * 528 <= AW, "R overflow mixer1"
        VT = vbf(o_V, 5 * 1024).rearrange("p (n d) -> p n d", d=2048)
        UT = vf32(o_V + 5120, 528)
        T2 = vf32(o_V + 5120 + 528, 528)
        PT = [ps[6][:, :].bitcast(BF16), ps[7][:, :].bitcast(BF16)]
        for half in range(2):
            if half == 0:
                g0 = 32
                hw = 512
                tiles = [(0, 512)]
                chunks = [(i * 128, 128) for i in range(4)]
                sgroups = [(0, 512, [0, 1, 2, 3])]
            else:
                g0 = 544
                hw = 528
                tiles = [(0, 264), (264, 264)]
                chunks = [(i * 128, 128) for i in range(4)] + [(512, 16)]
                sgroups = [(0, 512, [0, 1, 2, 3]), (512, 16, [4])]
            ln_zero()
            for nb2 in range(8):
                s = load_w(cinv[:, :, 2048 + nb2 * 256:2048 + (nb2 + 1) * 256], 16)
                for m in range(2):
                    oc = nb2 * 2 + m
                    for (t0, tn) in tiles:
                        gt = g0 + t0
                        b = nb()
                        mm_group(b, tn, s, m, 16, lambda k: HN[:, k, gt:gt + tn], lambda k: T("hn", k, gt, gt + tn))
                        q = sq_i[0] % 2
                        sq_i[0] += 1
                        S.add("dve", CALL("tensor_tensor", out=SQ[q][:, 0:tn], in0=ps[b][:, 0:tn], in1=RSTD[:, gt:gt + tn], op=ALU.mult),
                              reads=[("ps", b)] + T("rstd", 0, gt, gt + tn), writes=[("sq", q)])
                        S.add("act", CALL("activation",
                            out=V[:, oc, t0:t0 + tn], in_=SQ[q][:, 0:tn], func=AF.Gelu, bias=prm_col(P_CBI + 16 + oc), scale=1.0),
                            reads=[("sq", q), "prm"], writes=T("v", oc, t0, t0 + tn))
                        ln_accum(V[:, oc, t0:t0 + tn], T("v", oc, t0, t0 + tn), t0, tn)
            def fin_v(c, t0, tn, tmp, tmptok):
                S.add("act", CALL("activation",
                    out=VNB[:, c, t0:t0 + tn], in_=tmp, func=AF.Identity, bias=prm_col(P_CLB + c), scale=prm_col(P_CLG + c)),
                    reads=[tmptok, "prm"], writes=T("vnb", c, t0, t0 + tn))
                if t0 + tn == 528:
                    S.add("act", CALL("activation",
                        out=VS[:, c, :], in_=tmp[:, tn - 16:tn], func=AF.Identity, bias=prm_col(P_CLB + c), scale=prm_col(P_CLG + c)),
                        reads=[tmptok, "prm"], writes=[("vs", c)])
            layernorm_cols(V, "v", 16, tiles, fin_v)
            if half == 1:
                S.add("sp", CALL("dma_start", out=vSv, in_=VS), reads=[("vs", c) for c in range(16)], writes=["vS"], slot="vS")
            S.barrier()
            ti = 0
            for n, (c0, cw) in enumerate(chunks):
                for q4 in range(4):
                    pb = ti % 2
                    ti += 1
                    for qq in range(4):
                        oc = q4 * 4 + qq
                        S.add("pe", CALL("transpose",
                            out=PT[pb][0:cw, qq * 128:(qq + 1) * 128], in_=VNB[:, oc, c0:c0 + cw], identity=IDB),
                            reads=T("vnb", oc, c0, c0 + cw) + ["idb"], writes=[("ps", 6 + pb)])
                    if pb == 0:
                        S.add("act", CALL("copy", out=VT[0:cw, n, q4 * 512:(q4 + 1) * 512], in_=PT[pb][0:cw, 0:512]),
                              reads=[("ps", 6 + pb)], writes=[("vt", n, q4)])
                    else:
                        S.add("dve", CALL("tensor_copy", out=VT[0:cw, n, q4 * 512:(q4 + 1) * 512], in_=PT[pb][0:cw, 0:512]),
                              reads=[("ps", 6 + pb)], writes=[("vt", n, q4)])
            US = VNB
            for nb2 in range(8):
                s = load_w(cinv[:, :, nb2 * 256:(nb2 + 1) * 256], 16)
                for m in range(2):
                    oc = nb2 * 2 + m
                    h = oc // 2
                    for (t0, tn) in tiles:
                        gt = g0 + t0
                        b = nb()
                        mm_group(b, tn, s, m, 16, lambda k: HN[:, k, gt:gt + tn], lambda k: T("hn", k, gt, gt + tn))
                        q = sq_i[0] % 2
                        sq_i[0] += 1
                        S.add("dve", CALL("tensor_tensor", out=SQ[q][:, 0:tn], in0=ps[b][:, 0:tn], in1=RSTD[:, gt:gt + tn], op=ALU.mult),
                              reads=[("ps", b)] + T("rstd", 0, gt, gt + tn), writes=[("sq", q)])
                        S.add("act", CALL("activation",
                            out=UT[:, t0:t0 + tn], in_=SQ[q][:, 0:tn], func=AF.Gelu, bias=prm_col(P_CBI + oc), scale=1.0),
                            reads=[("sq", q), "prm"], writes=[("ut", t0)])
                    t2toks = []
                    for (s0, sn, cl) in sgroups:
                        b2 = nb()
                        first = True
                        for n in cl:
                            c0, cw = chunks[n]
                            lo = c0 - s0
                            rhs = WSTB[0:cw, h, 0:cw] if cw == 128 else WSSB[0:cw, h, 0:cw]
                            rtok = ["wstb"] if cw == 128 else [("wssb", h)]
                            S.add("pe", CALL("matmul",
                                ps[b2][:, lo:lo + cw], lhsT=VT[0:cw, n, oc * 128:(oc + 1) * 128], rhs=rhs, start=first, stop=True,
                                skip_group_check=True),
                                reads=[("vt", n, oc // 4)] + rtok, writes=[("ps", b2)])
                            first = False
                        if sn == 16:
                            S.add("dve", CALL("tensor_tensor",
                                out=T2[:, s0:s0 + sn], in0=ps[b2][:, 0:sn], in1=BSBS[:, h, :], op=ALU.add),
                                reads=[("ps", b2), ("bsbs", h)], writes=[("t2", s0)])
                        else:
                            bsrc = BROW[:, B_BS + h * 128:B_BS + (h + 1) * 128]
                            bb = bass.AP(bsrc.tensor, bsrc.offset, [list(bsrc.ap[0]), [0, 4], list(bsrc.ap[1])])
                            S.add("dve", CALL("tensor_tensor",
                                out=T2[:, s0:s0 + sn].rearrange("p (n t) -> p n t", t=128),
                                in0=ps[b2][:, 0:sn].rearrange("p (n t) -> p n t", t=128), in1=bb, op=ALU.add),
                                reads=[("ps", b2), "brow"], writes=[("t2", s0)])
                        t2toks.append(("t2", s0))
                    S.add("dve", CALL("tensor_tensor",
                        out=US[:, oc, 0:hw], in0=T2[:, 0:hw], in1=UT[:, 0:hw], op=ALU.mult),
                        reads=t2toks + [("ut", t0) for (t0, _) in tiles], writes=T("vnb", oc, 0, hw))
            if half == 1:
                half0_sq(32, 544)
                zero_rstd(544, W)
            for nb2 in range(8):
                s = load_w(coutv[:, :, nb2 * 256:(nb2 + 1) * 256], 16)
                for m in range(2):
                    oc = nb2 * 2 + m
                    for (t0, tn) in tiles:
                        b = nb()
                        mm_group(b, tn, s, m, 16, lambda k: US[:, k, t0:t0 + tn], lambda k: T("vnb", k, t0, t0 + tn))
                        gt = g0 + t0
                        S.add("dve", CALL("tensor_tensor",
                            out=X[:, oc, gt:gt + tn], in0=ps[b][:, 0:tn], in1=X[:, oc, gt:gt + tn], op=ALU.add),
                            reads=[("ps", b)] + T("x", oc, gt, gt + tn), writes=T("x", oc, gt, gt + tn))
                        post_mixer_epilogue(5, half, oc, gt, tn)
            if half == 1:
                flush_all()
            S.barrier()

    TL_ALL = [(0, 358), (358, 358), (716, 356)]
    TL_MAIN = [(32, 348), (380, 346), (726, 346)]

    def dump_x():
        for k in range(16):
            S.add("sp", CALL("dma_start", out=yTv[:, k, :], in_=X[:, k, 32:W]), reads=T("x", k, 32, W),
                  writes=[("y", k)], slot=("y", k % 4))

    def half0_sq(c_lo, c_hi):
        zero_rstd(c_lo, c_hi)
        for k in range(16):
            accum_sq(X[:, k, c_lo:c_hi], T("x", k, c_lo, c_hi), c_lo, c_hi - c_lo)
        flush_all()

    def run():
        zero_rstd(0, W)
        for (t0, tn) in TL_ALL:
            for k in range(16):
                if k % 2 == 0:
                    hn_xg(k, t0, tn, P_NG + 0 * 16)
                else:
                    S.add("dve", CALL("tensor_scalar", out=HN[:, k, t0:t0 + tn], in0=X[:, k, t0:t0 + tn], scalar1=prm_col(P_NG + k),
                                      scalar2=None, op0=ALU.mult),
                          reads=T("x", k, t0, t0 + tn) + ["prm"], writes=T("hn", k, t0, t0 + tn))
            for k in range(16):
                accum_sq(X[:, k, t0:t0 + tn], T("x", k, t0, t0 + tn), t0, tn)
        flush_all()
        rstd_defer(TL_ALL, per_tile=True)
        ffn(0, TL_ALL, next_gcol=P_NG + 1 * 16, hook=setup_spatial)
        S.barrier()
        if stop == "f0":
            return dump_x()
        rstd_defer(TL_ALL)
        mixer0()
        if stop == "m0":
            return dump_x()
        rstd_defer(TL_MAIN)
        ffn(1, TL_MAIN, next_gcol=P_NG + 3 * 16)
        if stop == "f1":
            return dump_x()
        rstd_defer(TL_MAIN)
        ffn(2, TL_MAIN, next_gcol=P_NG + 4 * 16)
        rstd_defer(TL_MAIN)
        mixer1()
        if stop == "m1":
            return dump_x()
        rstd_defer(TL_MAIN)
        ffn(3, TL_MAIN)
        S.barrier()
        YS = vf32(o_R, 12 * 512).rearrange("p (q t) -> p q t", t=512)
        cnt = [0]

        for (t0, tn) in TL_MAIN:
            bank = nb()
            S.add("pe", CALL("matmul", ps[bank][:, 0:tn], lhsT=ONES, rhs=RSTD[:, t0:t0 + tn], start=True, stop=True),
                  reads=["ones"] + T("rstd", 0, t0, t0 + tn), writes=[("ps", bank)])
            rstd_from(ps[bank][:, 0:tn], [("ps", bank)], RSTD[:, t0:t0 + tn], T("rstd", 0, t0, t0 + tn), 1.0 / D)
            for k in range(16):
                q = cnt[0] % 12
                cnt[0] += 1
                S.add("dve", CALL("scalar_tensor_tensor",
                    out=YS[:, q, 0:tn], in0=X[:, k, t0:t0 + tn], scalar=prm_col(P_FG + k), in1=RSTD[:, t0:t0 + tn],
                    op0=ALU.mult, op1=ALU.mult),
                    reads=T("x", k, t0, t0 + tn) + T("rstd", 0, t0, t0 + tn) + ["prm"], writes=[("ys", q)])
                S.add("sp", CALL("dma_start", out=yTv[:, k, t0 - 32:t0 - 32 + tn], in_=YS[:, q, 0:tn]),
                      reads=[("ys", q)], writes=[("y", k, t0)], slot=("y", q))

    run()
    finals = [k for k in S.slot_cnt if k[0] in ("y", "caP", "cbP", "caS", "cbS") or k == "vS"]
    stats = S.emit(nc, final_slots=finals)
    return nc, stats, S.n


def _pcol(v):
    v = np.asarray(v, np.float32)
    return np.ascontiguousarray(v.reshape(-1, 128).T)


def make_in_maps(inp):
    f32 = np.float32
    xp = np.asarray(inp["x_prompt"], f32)
    xs = np.asarray(inp["x_sample"], f32)
    sa = np.asarray(inp["state_conv_a"], f32)[0]
    sb = np.asarray(inp["state_conv_b"], f32)[0]
    shared = {
        "w1": np.ascontiguousarray(np.asarray(inp["ffn_w1"], f32).reshape(4, D, DFF)),
        "w3": np.ascontiguousarray(np.asarray(inp["ffn_w3"], f32).reshape(4, D, DFF)),
        "w2": np.ascontiguousarray(np.asarray(inp["ffn_w2"], f32).reshape(4, DFF, D)),
        "abin": np.ascontiguousarray(np.asarray(inp["ab_w_in"], f32)[0]),
        "about": np.ascontiguousarray(np.asarray(inp["ab_w_out"], f32)[0]),
        "cin": np.ascontiguousarray(np.asarray(inp["c_w_in"], f32)[0]),
        "cout": np.ascontiguousarray(np.asarray(inp["c_w_out"], f32)[0]),
    }
    prm = np.zeros((128, P_N), f32)
    ng = np.asarray(inp["norm_g"], f32).reshape(6, D)
    for i in range(6):
        prm[:, P_NG + i * 16:P_NG + (i + 1) * 16] = _pcol(ng[i])
    prm[:, P_FG:P_FG + 16] = _pcol(inp["final_g"])
    acw = np.asarray(inp["a_conv_w"], f32)[0]
    prm[:, P_ACW:P_ACW + 248] = acw.T.reshape(8, 128, 31).transpose(1, 0, 2).reshape(128, 248)
    prm[:, P_ACB:P_ACB + 8] = _pcol(np.asarray(inp["a_conv_b"])[0])
    prm[:, P_ALG:P_ALG + 8] = _pcol(np.asarray(inp["a_ln_g"])[0])
    prm[:, P_ALB:P_ALB + 8] = _pcol(np.asarray(inp["a_ln_b"])[0])
    bcw = np.asarray(inp["b_conv_w"], f32)[0]
    prm[:, P_BCW:P_BCW + 24] = bcw.T.reshape(8, 128, 3).transpose(1, 0, 2).reshape(128, 24)
    prm[:, P_CBI:P_CBI + 32] = _pcol(np.asarray(inp["c_b_in"])[0])
    prm[:, P_CLG:P_CLG + 16] = _pcol(np.asarray(inp["c_ln_g"])[0])
    prm[:, P_CLB:P_CLB + 16] = _pcol(np.asarray(inp["c_ln_b"])[0])
    cws = np.asarray(inp["c_w_s"], f32)[0]
    wst = np.ascontiguousarray(cws.transpose(2, 0, 1).reshape(128, 1024))
    cbs = np.asarray(inp["c_b_s"], f32)[0]
    brow = np.zeros((1, B_N), f32)
    brow[0, B_W00:B_W00 + 8] = cws[:, 0, 0]
    brow[0, B_BS0:B_BS0 + 8] = cbs[:, 0]
    brow[0, B_BS:B_BS + 1024] = cbs.reshape(-1)
    shared.update(prm=prm, wst=wst, brow=brow)
    maps = []
    for i in range(NCORES):
        b, half = i // 2, i % 2
        xt = np.zeros((D, W), f32)
        if half == 1:
            xt[:, 0:HALO] = xp[b, NPR - HALO:NPR, :].T
        xt[:, HALO:HALO + NPR] = xp[b, half * NPR:(half + 1) * NPR, :].T
        xt[:, HALO + NPR:W] = xs[i * NSM:(i + 1) * NSM, 0, :].T
        m = dict(shared)
        m["xT"] = xt
        m["sa"] = np.ascontiguousarray(sa[i * NSM:(i + 1) * NSM].transpose(2, 1, 0))
        m["sb"] = np.ascontiguousarray(sb[i * NSM:(i + 1) * NSM].transpose(2, 1, 0))
        maps.append(m)
    return maps


_CACHE = {}


def kernel(**inp):
    if "nc" not in _CACHE:
        _CACHE["nc"] = build_program(DEBUG_STOP)[0]
    nc = _CACHE["nc"]
    maps = make_in_maps(inp)
    res = run_bass_kernel_spmd(nc, maps, core_ids=list(range(NCORES)))
    R = res.results
    f32 = np.float32
    y_prompt = np.zeros((4, 2048, D), f32)
    y_sample = np.zeros((128, 1, D), f32)
    ca_p = np.zeros((1, 4, 30, 1024), f32)
    cb_p = np.zeros((1, 4, 2, 1024), f32)
    ca_s = np.zeros((1, 128, 30, 1024), f32)
    cb_s = np.zeros((1, 128, 2, 1024), f32)
    v_s = np.zeros((1, 128, 1, D), f32)
    for i in range(NCORES):
        b, half = i // 2, i % 2
        yt = np.asarray(R[i]["yT"], f32)
        y_prompt[b, half * NPR:(half + 1) * NPR, :] = yt[:, 0:NPR].T
        y_sample[i * NSM:(i + 1) * NSM, 0, :] = yt[:, NPR:].T
        if half == 1:
            ca_p[0, b] = np.asarray(R[i]["caP"], f32).T
            cb_p[0, b] = np.asarray(R[i]["cbP"], f32).T
        ca_s[0, i * NSM:(i + 1) * NSM] = np.asarray(R[i]["caS"], f32).transpose(2, 1, 0)
        cb_s[0, i * NSM:(i + 1) * NSM] = np.asarray(R[i]["cbS"], f32).transpose(2, 1, 0)
        v_s[0, i * NSM:(i + 1) * NSM, 0, :] = np.asarray(R[i]["vS"], f32).T
    return (y_prompt, y_sample, ca_p, cb_p, ca_s, cb_s, v_s)
```
